# Optimizing a Trainium2 kernel written in Bass

```python
import math
import jax, jax.numpy as jnp
from jax import lax
import numpy as np

D_MODEL = 1024
BATCH = 8
SEQ = 2048
DEPTH = 4
DEC_BATCH = 128
DEC_SEQ = 4
PAST_LEN = 2048
PAGE_SIZE = 128

N_MIXERS = 3
N_LAYERS_A = (DEPTH + 2) // 3
N_LAYERS_B = (DEPTH + 1) // 3
N_LAYERS_C = DEPTH // 3

POOL_WINDOWS = (2, 4, 8, 16)
POOL_GROUPS = len(POOL_WINDOWS)
POOL_GROUP_DIM = D_MODEL // POOL_GROUPS
POOL_STATE_ROWS = max(POOL_WINDOWS) - 1

DILATED_GROUPS = ((128, 1), (512, 4), (2048, 16))
N_DIL_GROUPS = len(DILATED_GROUPS)
HEADS_PER_GROUP = 4
HEAD_DIM = 64
N_HEADS_B = N_DIL_GROUPS * HEADS_PER_GROUP
ATTN_INNER = N_HEADS_B * HEAD_DIM
QUERY_BLOCK = 128
NUM_BUCKETS = 32
MAX_DISTANCE = 2048

GLA_HEADS = 4
GLA_DK = D_MODEL // (2 * GLA_HEADS)
GLA_DV = D_MODEL // GLA_HEADS
GLA_QK_WIDTH = GLA_HEADS * GLA_DK
GLA_V_WIDTH = GLA_HEADS * GLA_DV
GATE_RANK = 16
GATE_TAU = 16.0
GLA_CHUNK = 64
GLA_IN_WIDTH = 2 * GLA_QK_WIDTH + 2 * GLA_V_WIDTH + GATE_RANK

D_FF = 128 * ((8 * D_MODEL // 3 + 127) // 128)
CONV_WIDTH = 3

N_MOD = 6
EPS = 1e-6
NEG_INF = -1e30

kernel_name = 'hybrid_pool_dilated_gla_step'


def rms_norm(x, gain):
    xf = x.astype(jnp.float32)
    y = xf * lax.rsqrt(jnp.mean(xf * xf, axis=-1, keepdims=True) + EPS)
    return (y * gain.astype(jnp.float32)).astype(x.dtype)


def t5_bucket(dist):
    max_exact = NUM_BUCKETS // 2
    d_f = jnp.maximum(dist, 1).astype(jnp.float32)
    large = max_exact + (jnp.log(d_f / max_exact) / math.log(MAX_DISTANCE / max_exact)
                         * (NUM_BUCKETS - max_exact)).astype(jnp.int32)
    large = jnp.minimum(large, NUM_BUCKETS - 1)
    return jnp.where(dist < max_exact, dist, large)


def pool_mixer(u, past, pos0, w_group, layer_scale):
    B, T, _ = u.shape
    P = past.shape[1]
    u_all = jnp.concatenate([past.astype(u.dtype), u], axis=1)
    uf = u_all.astype(jnp.float32)
    cs = jnp.concatenate([jnp.zeros((B, 1, D_MODEL), jnp.float32), jnp.cumsum(uf, axis=1)], axis=1)
    rows = P + jnp.arange(T)
    pos = pos0 + jnp.arange(T)
    hi = cs[:, P + 1:]
    means = []
    for g, w in enumerate(POOL_WINDOWS):
        sl = slice(g * POOL_GROUP_DIM, (g + 1) * POOL_GROUP_DIM)
        lo = jnp.maximum(rows + 1 - w, 0)
        count = jnp.minimum(pos + 1, w).astype(jnp.float32)
        means.append((hi[:, :, sl] - cs[:, lo, sl]) / count[None, :, None])
    d = jnp.concatenate(means, axis=-1) - uf[:, P:]
    y = jnp.einsum('btgc,gce->btge', d.reshape(B, T, POOL_GROUPS, POOL_GROUP_DIM),
                   w_group.astype(jnp.float32))
    y = y.reshape(B, T, D_MODEL) * layer_scale.astype(jnp.float32)
    keep = min(POOL_STATE_ROWS, P + T)
    return y.astype(u.dtype), u_all[:, P + T - keep:]


def dilated_group_attention(q, k_all, v_all, p_past, dilation, window, bias):
    B, Tq, H, Dh = q.shape
    nk = window // dilation + 1
    offs = jnp.arange(nk) * dilation
    qb = math.gcd(Tq, QUERY_BLOCK)
    nb = Tq // qb
    q_blocks = q.reshape(B, nb, qb, H, Dh).swapaxes(0, 1)
    bias_f = bias.astype(jnp.float32)[None, :, None, :]

    def one_block(args):
        blk, qblk = args
        rows = p_past + blk * qb + jnp.arange(qb)
        idx = rows[:, None] - offs[None, :]
        valid = idx >= 0
        idx_c = jnp.maximum(idx, 0)
        kg = k_all[:, idx_c]
        vg = v_all[:, idx_c]
        logits = jnp.einsum('bqhd,bqkhd->bhqk', qblk, kg, preferred_element_type=jnp.float32)
        logits = logits * (HEAD_DIM ** -0.5) + bias_f
        logits = jnp.where(valid[None, None], logits, NEG_INF)
        lse = jax.nn.logsumexp(logits, axis=-1)
        p = jnp.exp(logits - lse[..., None])
        out = jnp.einsum('bhqk,bqkhd->bqhd', p.astype(vg.dtype), vg)
        return out, lse.transpose(0, 2, 1)

    outs, lses = lax.map(one_block, (jnp.arange(nb), q_blocks))
    out = outs.swapaxes(0, 1).reshape(B, Tq, H, Dh)
    lse = lses.swapaxes(0, 1).reshape(B, Tq, H)
    return out, lse


def dilated_attention_mixer(h, pasts, w_in, w_out, rel_bias):
    B, T, _ = h.shape
    qkv = (h @ w_in).reshape(B, T, 3, N_DIL_GROUPS, HEADS_PER_GROUP, HEAD_DIM)
    outs, lses, new_bufs = [], [], []
    for g, (window, dil) in enumerate(DILATED_GROUPS):
        past = pasts[g].astype(h.dtype)
        P = past.shape[1]
        kv_all = jnp.concatenate([past, qkv[:, :, 1:, g].astype(h.dtype)], axis=1)
        buckets = t5_bucket(jnp.arange(window // dil + 1) * dil)
        bias = rel_bias[buckets][:, g * HEADS_PER_GROUP:(g + 1) * HEADS_PER_GROUP].T
        o, lse = dilated_group_attention(qkv[:, :, 0, g], kv_all[:, :, 0], kv_all[:, :, 1], P, dil, window, bias)
        outs.append(o)
        lses.append(lse)
        keep = min(window, P + T)
        new_bufs.append(kv_all[:, P + T - keep:])
    alpha = jax.nn.softmax(jnp.stack(lses, axis=2), axis=2)
    o_all = jnp.stack(outs, axis=2) * alpha[..., None].astype(h.dtype)
    return o_all.reshape(B, T, ATTN_INNER) @ w_out, new_bufs


def gla_mixer(h, S0, w_in, w_gate_up, b_gate, norm_gain, w_out):
    B, T, _ = h.shape
    proj = h @ w_in
    q, k, v, r, gd = jnp.split(proj, [GLA_QK_WIDTH, 2 * GLA_QK_WIDTH, 2 * GLA_QK_WIDTH + GLA_V_WIDTH,
                                      2 * GLA_QK_WIDTH + 2 * GLA_V_WIDTH], axis=-1)
    q = q.reshape(B, T, GLA_HEADS, GLA_DK).astype(jnp.float32) * (GLA_DK ** -0.5)
    k = k.reshape(B, T, GLA_HEADS, GLA_DK).astype(jnp.float32)
    v = v.reshape(B, T, GLA_HEADS, GLA_DV).astype(jnp.float32)
    log_a = jax.nn.log_sigmoid((gd @ w_gate_up + b_gate).astype(jnp.float32)) / GATE_TAU
    log_a = log_a.reshape(B, T, GLA_HEADS, GLA_DK)
    C = math.gcd(T, GLA_CHUNK)
    nc = T // C

    def to_chunks(a):
        return a.reshape(B, nc, C, *a.shape[2:]).swapaxes(0, 1)

    causal = jnp.tril(jnp.ones((C, C), dtype=bool))

    def step(S, inp):
        qc, kc, vc, gc = inp
        b = jnp.cumsum(gc, axis=1)
        o_inter = jnp.einsum('bthk,bhkv->bthv', qc * jnp.exp(b), S)
        diff = b[:, :, None] - b[:, None, :]
        decay = jnp.exp(jnp.where(causal[None, :, :, None, None], diff, -jnp.inf))
        att = jnp.einsum('bthk,bshk,btshk->bhts', qc, kc, decay)
        o_intra = jnp.einsum('bhts,bshv->bthv', att, vc)
        b_end = b[:, -1]
        S_new = jnp.exp(b_end)[..., None] * S + jnp.einsum('bshk,bshv->bhkv', kc * jnp.exp(b_end[:, None] - b), vc)
        return S_new, o_inter + o_intra

    S_fin, o = lax.scan(step, S0.astype(jnp.float32), (to_chunks(q), to_chunks(k), to_chunks(v), to_chunks(log_a)))
    o = o.swapaxes(0, 1).reshape(B, T, GLA_HEADS, GLA_DV)
    o = o * lax.rsqrt(jnp.mean(o * o, axis=-1, keepdims=True) + EPS)
    o = o.reshape(B, T, GLA_V_WIDTH) * norm_gain.astype(jnp.float32)
    out = (o.astype(h.dtype) * jax.nn.silu(r)) @ w_out
    return out, S_fin


def conv_ffn(h, past, w_in, conv_w, conv_b, w_down):
    T = h.shape[1]
    g, u = jnp.split(h @ w_in, 2, axis=-1)
    g_all = jnp.concatenate([past.astype(g.dtype), g], axis=1)
    gc = conv_w[0] * g_all[:, 0:T] + conv_w[1] * g_all[:, 1:T + 1] + conv_w[2] * g_all[:, 2:T + 2] + conv_b
    y = (jax.nn.gelu(gc, approximate=False) * u) @ w_down
    return y, g_all[:, -(CONV_WIDTH - 1):]


def trunk(x, c, pos0, pool_past, win_past1, win_past2, win_past3, gla_past, conv_past,
          w_ada, b_ada, norm_gain, final_gain, rel_bias, pool_w, pool_scale,
          attn_w_in, attn_w_out, gla_w_in, gla_w_gate_up, gla_b_gate, gla_norm_gain, gla_w_out,
          ffn_w_in, ffn_conv_w, ffn_conv_b, ffn_w_down):
    B = x.shape[0]
    c_act = jax.nn.silu(c)
    new_pool, new_w1, new_w2, new_w3, new_gla, new_conv = [], [], [], [], [], []
    for i in range(DEPTH):
        kind, j = i % N_MIXERS, i // N_MIXERS
        mod = (c_act @ w_ada[i] + b_ada[i]).reshape(B, N_MOD, 1, D_MODEL)
        shift1, scale1, gate1, shift2, scale2, gate2 = (mod[:, m] for m in range(N_MOD))
        h = rms_norm(x, norm_gain[i, 0]) * (1 + scale1) + shift1
        if kind == 0:
            mix, st = pool_mixer(h, pool_past[j], pos0, pool_w[j], pool_scale[j])
            new_pool.append(st)
        elif kind == 1:
            mix, bufs = dilated_attention_mixer(h, (win_past1[j], win_past2[j], win_past3[j]),
                                                attn_w_in[j], attn_w_out[j], rel_bias)
            new_w1.append(bufs[0])
            new_w2.append(bufs[1])
            new_w3.append(bufs[2])
        else:
            mix, st = gla_mixer(h, gla_past[j], gla_w_in[j], gla_w_gate_up[j], gla_b_gate[j],
                                gla_norm_gain[j], gla_w_out[j])
            new_gla.append(st)
        x = x + gate1 * mix
        h = rms_norm(x, norm_gain[i, 1]) * (1 + scale2) + shift2
        f, st = conv_ffn(h, conv_past[i], ffn_w_in[i], ffn_conv_w[i], ffn_conv_b[i], ffn_w_down[i])
        new_conv.append(st)
        x = x + gate2 * f
    y = rms_norm(x, final_gain)
    return (y, jnp.stack(new_pool), jnp.stack(new_w1), jnp.stack(new_w2), jnp.stack(new_w3),
            jnp.stack(new_gla), jnp.stack(new_conv))


def setup_inputs(seed: int = 0) -> dict:
    key = jax.random.key(seed)
    ks = jax.random.split(key, 32)

    def nrm(k, shape, s):
        return jax.random.normal(k, shape, jnp.float32) * s

    win_rows = [min(w, PAST_LEN) for (w, _) in DILATED_GROUPS]
    return {
        'x_prompt': nrm(ks[0], (BATCH, SEQ, D_MODEL), 1.0),
        'x_sample': nrm(ks[1], (DEC_BATCH, DEC_SEQ, D_MODEL), 1.0),
        'state_pool': nrm(ks[2], (N_LAYERS_A, DEC_BATCH, POOL_STATE_ROWS, D_MODEL), 1.0),
        'cache_win_g1': nrm(ks[3], (N_LAYERS_B, DEC_BATCH, win_rows[0], 2, HEADS_PER_GROUP, HEAD_DIM), 1.0),
        'cache_win_g2': nrm(ks[4], (N_LAYERS_B, DEC_BATCH, win_rows[1], 2, HEADS_PER_GROUP, HEAD_DIM), 1.0),
        'cache_win_g3': nrm(ks[5], (N_LAYERS_B, DEC_BATCH, win_rows[2], 2, HEADS_PER_GROUP, HEAD_DIM), 1.0),
        'state_gla': nrm(ks[6], (N_LAYERS_C, DEC_BATCH, GLA_HEADS, GLA_DK, GLA_DV), 1.0),
        'state_ffn_conv': nrm(ks[7], (DEPTH, DEC_BATCH, CONV_WIDTH - 1, D_FF), 1.0),
        'c_prompt': nrm(ks[8], (BATCH, D_MODEL), 1.0),
        'c_sample': nrm(ks[9], (DEC_BATCH, D_MODEL), 1.0),
        'w_ada': nrm(ks[10], (DEPTH, D_MODEL, N_MOD * D_MODEL), 0.3 * D_MODEL ** -0.5),
        'b_ada': nrm(ks[11], (DEPTH, N_MOD * D_MODEL), 0.02),
        'norm_gain': 1.0 + nrm(ks[12], (DEPTH, 2, D_MODEL), 0.05),
        'final_gain': 1.0 + nrm(ks[13], (D_MODEL,), 0.05),
        'rel_bias': nrm(ks[14], (NUM_BUCKETS, N_HEADS_B), 0.5),
        'pool_w': nrm(ks[15], (N_LAYERS_A, POOL_GROUPS, POOL_GROUP_DIM, POOL_GROUP_DIM), POOL_GROUP_DIM ** -0.5),
        'pool_scale': 1.0 + nrm(ks[16], (N_LAYERS_A, D_MODEL), 0.05),
        'attn_w_in': nrm(ks[17], (N_LAYERS_B, D_MODEL, 3 * ATTN_INNER), D_MODEL ** -0.5),
        'attn_w_out': nrm(ks[18], (N_LAYERS_B, ATTN_INNER, D_MODEL), ATTN_INNER ** -0.5),
        'gla_w_in': nrm(ks[19], (N_LAYERS_C, D_MODEL, GLA_IN_WIDTH), D_MODEL ** -0.5),
        'gla_w_gate_up': nrm(ks[20], (N_LAYERS_C, GATE_RANK, GLA_QK_WIDTH), GATE_RANK ** -0.5),
        'gla_b_gate': nrm(ks[21], (N_LAYERS_C, GLA_QK_WIDTH), 0.1),
        'gla_norm_gain': 1.0 + nrm(ks[22], (N_LAYERS_C, GLA_V_WIDTH), 0.05),
        'gla_w_out': nrm(ks[23], (N_LAYERS_C, GLA_V_WIDTH, D_MODEL), GLA_V_WIDTH ** -0.5),
        'ffn_w_in': nrm(ks[24], (DEPTH, D_MODEL, 2 * D_FF), D_MODEL ** -0.5),
        'ffn_conv_w': nrm(ks[25], (DEPTH, CONV_WIDTH, D_FF), CONV_WIDTH ** -0.5),
        'ffn_conv_b': nrm(ks[26], (DEPTH, D_FF), 0.02),
        'ffn_w_down': nrm(ks[27], (DEPTH, D_FF, D_MODEL), D_FF ** -0.5),
    }


def reference(x_prompt, x_sample, state_pool, cache_win_g1, cache_win_g2, cache_win_g3, state_gla, state_ffn_conv,
              c_prompt, c_sample, w_ada, b_ada, norm_gain, final_gain, rel_bias, pool_w, pool_scale,
              attn_w_in, attn_w_out, gla_w_in, gla_w_gate_up, gla_b_gate, gla_norm_gain, gla_w_out,
              ffn_w_in, ffn_conv_w, ffn_conv_b, ffn_w_down):
    weights = (w_ada, b_ada, norm_gain, final_gain, rel_bias, pool_w, pool_scale,
               attn_w_in, attn_w_out, gla_w_in, gla_w_gate_up, gla_b_gate, gla_norm_gain, gla_w_out,
               ffn_w_in, ffn_conv_w, ffn_conv_b, ffn_w_down)
    dt = x_prompt.dtype
    empty_pool = jnp.zeros((N_LAYERS_A, BATCH, 0, D_MODEL), dt)
    empty_win = jnp.zeros((N_LAYERS_B, BATCH, 0, 2, HEADS_PER_GROUP, HEAD_DIM), dt)
    zero_gla = jnp.zeros((N_LAYERS_C, BATCH, GLA_HEADS, GLA_DK, GLA_DV), jnp.float32)
    zero_conv = jnp.zeros((DEPTH, BATCH, CONV_WIDTH - 1, D_FF), dt)
    (y_prompt, pool_p, win1_p, win2_p, win3_p, gla_p, conv_p) = trunk(
        x_prompt, c_prompt, 0, empty_pool, empty_win, empty_win, empty_win, zero_gla, zero_conv, *weights)
    (y_sample, pool_s, win1_s, win2_s, win3_s, gla_s, conv_s) = trunk(
        x_sample, c_sample, PAST_LEN, state_pool, cache_win_g1, cache_win_g2, cache_win_g3, state_gla,
        state_ffn_conv, *weights)
    return (y_prompt, y_sample, pool_p, pool_s, win1_p, win1_s, win2_p, win2_s, win3_p, win3_s,
            gla_p, gla_s, conv_p, conv_s)
```

```python
import math
from contextlib import ExitStack
import numpy as np
import concourse.bass as bass
import concourse.mybir as mybir
from concourse.bass_utils import run_bass_kernel_spmd

F32 = mybir.dt.float32
BF16 = mybir.dt.bfloat16
AF = mybir.ActivationFunctionType
ALU = mybir.AluOpType
AX = mybir.AxisListType

NCORES = 8
D = 1024
KC = 8
NP = 2048
NS = 64
NT = NP + NS
HOFF = 16
DFF = 2816
NFC = 22
DEPTH = 4
EPS = 1e-6
NEG = -30000.0
TILES = [(0, 512), (512, 512), (1024, 512), (1536, 512), (2048, 64)]
POOL_W = (2, 4, 8, 16)
DIL = ((128, 1), (512, 4), (2048, 16))
PV_NG, PV_FG, PV_PSC, PV_GNG, PV_BADA, PV_CW, PV_CB, PV_N = 0, 64, 72, 88, 96, 288, 552, 640
FFN_PIECES = [(0, 4), (4, 4), (8, 4), (12, 4), (16, 3), (19, 3)]
NSLOT = 12
import os
DBG = os.environ.get('KDBG', '').split(',')


class Prog:
    def __init__(self):
        self.ops = []
        self.lastw = {}
        self.rds = {}

    def add(self, eng, fn, reads=(), writes=(), dma=False):
        oid = len(self.ops)
        deps = set()
        for k in reads:
            w = self.lastw.get(k)
            if w is not None:
                deps.add(w)
            if isinstance(k, tuple) and k[0] == 'ps':
                r = self.rds.get(k)
                if r:
                    for e2, o2 in r[0].items():
                        if e2 != eng:
                            deps.add(o2)
        for k in writes:
            w = self.lastw.get(k)
            if w is not None:
                deps.add(w)
            r = self.rds.get(k)
            if r:
                deps.update(r[0].values())
                deps.update(r[1])
        for k in writes:
            self.lastw[k] = oid
            self.rds[k] = ({}, [])
        for k in reads:
            r = self.rds.setdefault(k, ({}, []))
            if dma:
                r[1].append(oid)
            else:
                r[0][eng] = oid
        deps.discard(oid)
        self.ops.append(dict(id=oid, eng=eng, fn=fn, dma=dma, deps=deps, signal=False, val=0, slot=0))
        return oid

    def emit(self, nc, es):
        ops = self.ops
        engs = ['pe', 'act', 'dve', 'pool', 'sp']
        for op in ops:
            for d in op['deps']:
                dop = ops[d]
                if dop['dma']:
                    continue
                if dop['eng'] == 'pe' and op['eng'] == 'pe' and not op['dma']:
                    continue
                dop['signal'] = True
        cnt = {e: 0 for e in engs}
        dcnt = {e: 0 for e in engs}
        for op in ops:
            e = op['eng']
            if op['dma']:
                i = dcnt[e]
                op['slot'] = i % NSLOT
                op['val'] = 16 * (i // NSLOT + 1)
                dcnt[e] += 1
            elif op['signal']:
                cnt[e] += 1
                op['val'] = cnt[e]
        csem = {e: es.enter_context(nc.semaphore("c_" + e)) for e in ['pe', 'act', 'dve', 'pool']}
        dsem = {}
        for e in ['act', 'pool', 'sp']:
            if dcnt[e] > 0:
                dsem[e] = [es.enter_context(nc.semaphore("d_%s_%d" % (e, s))) for s in range(min(NSLOT, dcnt[e]))]
        block = es.enter_context(nc.Block())
        per = {e: [op for op in ops if op['eng'] == e] for e in engs}

        def run_engine(ename, eobj):
            wm = {}

            def wait(sem, key, val):
                if wm.get(key, 0) >= val:
                    return
                wm[key] = val
                eobj.wait_ge(sem, val)

            for op in per[ename]:
                for d in sorted(op['deps']):
                    dop = ops[d]
                    if dop['dma']:
                        wait(dsem[dop['eng']][dop['slot']], ('d', dop['eng'], dop['slot']), dop['val'])
                    else:
                        if dop['eng'] == 'pe' and ename == 'pe' and not op['dma']:
                            continue
                        wait(csem[dop['eng']], ('c', dop['eng']), dop['val'])
                if op['dma']:
                    if op['val'] > 16:
                        wait(dsem[ename][op['slot']], ('d', ename, op['slot']), op['val'] - 16)
                    ins = op['fn'](eobj)
                    ins.then_inc(dsem[ename][op['slot']], 16)
                else:
                    ins = op['fn'](eobj)
                    if op['signal']:
                        ins.then_inc(csem[ename], 1)
            if ename == 'sp':
                for e2, sems in dsem.items():
                    n = dcnt[e2]
                    for s, sem in enumerate(sems):
                        k = (n - s + NSLOT - 1) // NSLOT
                        if k > 0:
                            eobj.wait_ge(sem, 16 * k)
                for e2 in ['pe', 'act', 'dve', 'pool']:
                    if cnt[e2] > 0:
                        eobj.wait_ge(csem[e2], cnt[e2])

        @block.tensor
        def _(e):
            run_engine('pe', e)

        @block.scalar
        def _(e):
            run_engine('act', e)

        @block.vector
        def _(e):
            run_engine('dve', e)

        @block.gpsimd
        def _(e):
            run_engine('pool', e)

        @block.sync
        def _(e):
            run_engine('sp', e)


def build_program(stage=99):
    nc = bass.Bass("TRN2", target_bir_lowering=False)
    P = Prog()
    es = ExitStack()

    def din(name, shape, dt=F32):
        return nc.dram_tensor(name, list(shape), dt, kind="ExternalInput").ap()

    def dout(name, shape, dt=F32):
        return nc.dram_tensor(name, list(shape), dt, kind="ExternalOutput").ap()

    xp = din("xp", [NP, D])
    xs = din("xs", [16, 4, D])
    spool = din("spool", [2, 16, 15, D])
    cw = [din("cw1", [16, 128, 512]), din("cw2", [16, 512, 512]), din("cw3", [16, 2048, 512])]
    sgla = din("sgla", [16, 4, 128, 256])
    sconv = din("sconv", [DEPTH, 16, 2, DFF])
    cvec = din("cvec", [17, D])
    w_ada = din("w_ada", [DEPTH, D, 6 * D])
    pvec = din("pvec", [PV_N, 128])
    pool_w = din("pool_w", [2, 4, 256, 256])
    attn_w_in = din("attn_w_in", [D, 2304])
    attn_w_out = din("attn_w_out", [768, D])
    gla_w_in = din("gla_w_in", [D, 3088])
    gla_wg = din("gla_wg", [17, 512])
    gla_w_out = din("gla_w_out", [D, D])
    ffn_w_in = din("ffn_w_in", [DEPTH, D, 2 * DFF])
    ffn_w_down = din("ffn_w_down", [DEPTH, DFF, D])
    ident_d = din("ident_in", [128, 128])
    invc_d = din("invc_in", [128, 4, 16])
    mbp_d = din("mbp", [2, 128, 1536])
    mbs_d = din("mbs", [128, 144])
    mbn_d = din("mbn", [64, 768])
    gtri_d = din("gtri", [128, 128])
    gmsk_d = din("gmsk", [128, 128])
    gtris_d = din("gtris", [64, 64])
    gmsks_d = din("gmsks", [64, 64])
    gmkb_d = din("gmkb", [64, 16])

    yp = dout("yp", [NP, D])
    ys = dout("ys", [16, 4, D])
    pool_p = dout("pool_p", [2, 15, D])
    pool_s = dout("pool_s", [2, 16, 15, D])
    win_p = [dout("win1_p", [128, 512]), dout("win2_p", [512, 512]), dout("win3_p", [2048, 512])]
    win_s = [dout("win1_s", [16, 128, 512]), dout("win2_s", [16, 512, 512]), dout("win3_s", [16, 2048, 512])]
    gla_p = dout("gla_p", [4, 128, 256])
    gla_s = dout("gla_s", [16, 4, 128, 256])
    conv_p = dout("conv_p", [DEPTH, 2, DFF])
    conv_s = dout("conv_s", [DEPTH, 16, 2, DFF])

    def sb(name, shape, dt):
        return es.enter_context(nc.sbuf_tensor(name, list(shape), dt))

    XT = sb("XT", [128, KC, NT], F32)
    hT = sb("hT", [128, KC, HOFF + NT], BF16)
    modT = sb("modT", [128, DEPTH, 48, 17], F32)
    PV = sb("PV", [128, PV_N], F32)
    cT = sb("cT", [128, KC, 17], BF16)
    ident = sb("ident", [128, 128], F32)
    ones_bf = sb("ones_bf", [128, 128], BF16)
    ident_bf = sb("ident_bf", [128, 128], BF16)
    zeros_bf = sb("zeros_bf", [128, 512], BF16)
    epsT = sb("epsT", [128, 1], F32)
    invc = sb("invc", [128, 4, 16], F32)
    hlast = sb("hlast", [128, KC, 15 + NS], F32)
    ARENA_W = 22400
    arena = sb("arena", [128, ARENA_W], F32)
    psb = [es.enter_context(nc.psum_tensor("ps%d" % i, [128, 512], F32)) for i in range(8)]

    arena_state = {'gen': 0, 'off': 0}

    def arena_reset():
        g = arena_state['gen']
        P.add('dve', lambda e: e.memset(epsT[:], EPS), reads=[], writes=[('arena', g), ('arena', g + 1), 'epsT'])
        arena_state['gen'] = g + 1
        arena_state['off'] = 0

    def carve(nwords_f32):
        o = arena_state['off']
        assert o + nwords_f32 <= ARENA_W, ("arena overflow", o, nwords_f32)
        arena_state['off'] = o + nwords_f32
        return arena[:, o:o + nwords_f32]

    def carve_f32(shape):
        n = int(np.prod(shape[1:]))
        v = carve(n)
        if len(shape) == 3:
            v = v.rearrange("p (a b) -> p a b", b=shape[2])
        elif len(shape) == 4:
            v = v.rearrange("p (a b c) -> p a b c", b=shape[2], c=shape[3])
        return v

    def carve_bf16(shape):
        n = int(np.prod(shape[1:]))
        assert n % 2 == 0
        v = carve(n // 2).bitcast(BF16)
        if len(shape) == 3:
            v = v.rearrange("p (a b) -> p a b", b=shape[2])
        elif len(shape) == 4:
            v = v.rearrange("p (a b c) -> p a b c", b=shape[2], c=shape[3])
        return v

    def AK():
        return ('arena', arena_state['gen'])

    ps_rr = {'i': 0, 'held': None}

    def psum():
        i = ps_rr['i']
        if i == ps_rr['held']:
            i = (i + 1) % 8
        ps_rr['i'] = (i + 1) % 8
        return psb[i], ('ps', i)

    def psum_hold():
        pb, k = psum()
        ps_rr['held'] = k[1]
        return pb, k

    def psum_release():
        ps_rr['held'] = None

    def op(eng, fn, reads=(), writes=(), arena_use=True):
        r = list(reads)
        if arena_use:
            r.append(AK())
        return P.add(eng, fn, r, list(writes))

    def dma(eng, out_ap, in_ap, reads=(), writes=(), arena_use=True):
        r = list(reads)
        if arena_use:
            r.append(AK())
        return P.add(eng, lambda e: e.dma_start(out=out_ap, in_=in_ap), r, list(writes), dma=True)

    def pv(col, n=1):
        return PV[:, col:col + n]

    def transpose_to(ps_ap, in_ap, k):
        return lambda e: e.transpose(ps_ap, in_ap, ident[0:k, 0:k])

    arena_reset()
    for g, (win, dil) in enumerate(DIL):
        if g < 2:
            dma('sp', win_s[g][:, 0:win - 4, :], cw[g][:, 4:win, :], arena_use=False)
    dma('sp', ident[:], ident_d[:, :], writes=['ident'], arena_use=False)
    dma('sp', invc[:], invc_d[:, :, :], writes=['invc'], arena_use=False)
    op('dve', lambda e: e.memset(ones_bf[:], 1.0), writes=['ones'], arena_use=False)
    op('dve', lambda e: e.memset(zeros_bf[:], 0.0), writes=['zeros'], arena_use=False)
    op('dve', lambda e: e.tensor_copy(out=ident_bf[:], in_=ident[:]), reads=['ident'], writes=['identbf'], arena_use=False)

    pst = carve_f32([128, 5, 128])
    dma('sp', pst, pvec.rearrange("(t p) f -> p t f", p=128), writes=['pst'])
    ctile = carve_f32([17, D])[0:17]
    dma('sp', ctile, cvec[:, :], writes=['ctile'])
    for t0 in (0, 4):
        nt = min(4, 5 - t0)
        pst_ps, pk = psum()
        for t in range(nt):
            op('pe', transpose_to(pst_ps[:, t * 128:(t + 1) * 128], pst[:, t0 + t, :], 128), reads=['pst', 'ident'], writes=[pk])
        op('dve', lambda e, a=pst_ps, t0=t0, nt=nt: e.tensor_copy(out=PV[:, t0 * 128:(t0 + nt) * 128], in_=a[:, 0:nt * 128]),
           reads=[pk], writes=['PV'])
    op('act', lambda e: e.activation(out=ctile, in_=ctile, func=AF.Silu), reads=['ctile'], writes=['ctile'])
    c_ps, ck = psum()
    for kc in range(KC):
        op('pe', transpose_to(c_ps[:, kc * 17:(kc + 1) * 17], ctile[:, kc * 128:(kc + 1) * 128], 17), reads=['ctile', 'ident'], writes=[ck])
    op('dve', lambda e: e.tensor_copy(out=cT[:], in_=c_ps[:, 0:KC * 17].rearrange("p (a b) -> p a b", b=17)), reads=[ck], writes=['cT'])

    xst = [carve_f32([128, D]) for _ in range(3)]
    for blk in range(17):
        st = xst[blk % 3]
        sk = ('xst', blk % 3)
        if blk < 16:
            rows = 128
            dma('sp', st, xp[blk * 128:(blk + 1) * 128, :], writes=[sk])
        else:
            rows = NS
            for t in range(4):
                dma('sp', st[16 * t:16 * t + 16, :], xs[:, t, :], writes=[sk])
        for q in range(2):
            x_ps, xk = psum()
            for k4 in range(4):
                kc = q * 4 + k4
                op('pe', transpose_to(x_ps[:, k4 * 128:k4 * 128 + rows], st[0:rows, kc * 128:(kc + 1) * 128], rows),
                   reads=[sk, 'ident'], writes=[xk])
            op('dve' if q == 0 else 'act',
               (lambda e, a=x_ps, q=q, blk=blk, rows=rows: e.tensor_copy(
                   out=XT[:, q * 4:q * 4 + 4, blk * 128:blk * 128 + rows],
                   in_=a[:, :].rearrange("p (a b) -> p a b", b=128)[:, :, 0:rows])) if q == 0 else
               (lambda e, a=x_ps, q=q, blk=blk, rows=rows: e.activation(
                   out=XT[:, q * 4:q * 4 + 4, blk * 128:blk * 128 + rows],
                   in_=a[:, :].rearrange("p (a b) -> p a b", b=128)[:, :, 0:rows], func=AF.Copy)),
               reads=[xk], writes=[('XT', min(blk // 4, 4))])

    wad = [carve_bf16([128, KC, 1024]) for _ in range(2)]
    pi = 0
    for i in range(DEPTH):
        for m in range(6):
            wt = wad[pi % 2]
            wk = ('wad', pi % 2)
            pi += 1
            dma('pool', wt, w_ada[i, :, m * 1024:(m + 1) * 1024].rearrange("(k p) n -> p k n", p=128), writes=[wk])
            m_ps, mk = psum()
            for oc in range(8):
                for kc in range(KC):
                    op('pe', lambda e, a=m_ps, wt=wt, oc=oc, kc=kc: e.matmul(
                        a[:, oc * 17:(oc + 1) * 17], lhsT=wt[:, kc, oc * 128:(oc + 1) * 128], rhs=cT[:, kc, :],
                        start=(kc == 0), stop=(kc == KC - 1)), reads=[wk, 'cT'], writes=[mk])
            mo = modT[:, i, m * 8:(m + 1) * 8, :]
            bcol = PV_BADA + i * 48 + m * 8
            op('dve', lambda e, a=m_ps, mo=mo, bcol=bcol: e.tensor_tensor(
                out=mo, in0=a[:, 0:136].rearrange("p (a b) -> p a b", b=17),
                in1=PV[:, bcol:bcol + 8].unsqueeze(2).broadcast_to([128, 8, 17]), op=ALU.add),
               reads=[mk, 'PV'], writes=[('mod', i, m)])
            if m in (1, 4):
                gcol = PV_NG + i * 16 + (0 if m == 1 else 8)
                op('dve', lambda e, mo=mo, gcol=gcol: e.scalar_tensor_tensor(
                    out=mo, in0=mo, scalar=1.0, in1=PV[:, gcol:gcol + 8].unsqueeze(2).broadcast_to([128, 8, 17]),
                    op0=ALU.add, op1=ALU.mult), reads=[('mod', i, m), 'PV'], writes=[('mod', i, m)])
            if m == 2 and i % 3 == 0:
                scol = PV_PSC + (i // 3) * 8
                op('dve', lambda e, mo=mo, scol=scol: e.tensor_tensor(
                    out=mo, in0=mo, in1=PV[:, scol:scol + 8].unsqueeze(2).broadcast_to([128, 8, 17]), op=ALU.mult),
                   reads=[('mod', i, m), 'PV'], writes=[('mod', i, m)])

    def norm_mod(i, which, want_last=False, final=False):
        mS = 0 if which == 0 else 3
        mA = mS + 1
        sq = carve_bf16([128, KC, 512])
        rstd = [carve_f32([128, 512]) for _ in range(2)]
        t1 = [carve_f32([128, 512]) for _ in range(3)]
        cnt = 0
        for ti, (c0, n) in enumerate(TILES):
            xk = ('XT', ti)
            for kc in range(KC):
                op('act', lambda e, kc=kc, c0=c0, n=n: e.activation(out=sq[:, kc, 0:n], in_=XT[:, kc, c0:c0 + n], func=AF.Square),
                   reads=[xk], writes=[('sq', kc)])
            s_ps, sk = psum()
            for kc in range(KC):
                op('pe', lambda e, a=s_ps, kc=kc, n=n: e.matmul(a[:, 0:n], lhsT=ones_bf[:], rhs=sq[:, kc, 0:n],
                                                                 start=(kc == 0), stop=(kc == KC - 1)),
                   reads=[('sq', kc), 'ones'], writes=[sk])
            rs = rstd[ti % 2]
            rk = ('rstd', ti % 2)
            op('act', lambda e, a=s_ps, rs=rs, n=n: e.activation(out=rs[:, 0:n], in_=a[:, 0:n], func=AF.Ln, bias=epsT[:, 0:1], scale=1.0 / D),
               reads=[sk, 'epsT'], writes=[rk])
            op('act', lambda e, rs=rs, n=n: e.activation(out=rs[:, 0:n], in_=rs[:, 0:n], func=AF.Exp, scale=-0.5),
               reads=[rk], writes=[rk])
            for kc in range(KC):
                tt = t1[cnt % 3]
                tk = ('t1', cnt % 3)
                cnt += 1
                op('dve', lambda e, tt=tt, kc=kc, c0=c0, n=n, rs=rs: e.tensor_tensor(out=tt[:, 0:n], in0=XT[:, kc, c0:c0 + n], in1=rs[:, 0:n], op=ALU.mult),
                   reads=[xk, rk], writes=[tk])
                if final:
                    op('act', lambda e, tt=tt, kc=kc, c0=c0, n=n: e.activation(
                        out=XT[:, kc, c0:c0 + n], in_=tt[:, 0:n], func=AF.Identity, scale=PV[:, PV_FG + kc:PV_FG + kc + 1]),
                       reads=[tk, 'PV'], writes=[xk])
                    continue
                if ti < 4:
                    Aap = modT[:, i, mA * 8 + kc, 0:1]
                    Sap = modT[:, i, mS * 8 + kc, 0:1]
                    op('act', lambda e, tt=tt, kc=kc, c0=c0, n=n, Aap=Aap, Sap=Sap: e.activation(
                        out=hT[:, kc, HOFF + c0:HOFF + c0 + n], in_=tt[:, 0:n], func=AF.Identity, bias=Sap, scale=Aap),
                       reads=[tk, ('mod', i, mA), ('mod', i, mS)], writes=[('hT', ti)])
                    if want_last and ti == 3:
                        op('dve', lambda e, tt=tt, kc=kc, Aap=Aap, Sap=Sap: e.tensor_scalar(
                            out=hlast[:, kc, 0:15], in0=tt[:, 497:512], scalar1=Aap, scalar2=Sap, op0=ALU.mult, op1=ALU.add),
                           reads=[tk, ('mod', i, mA), ('mod', i, mS)], writes=['hlast'])
                else:
                    Ab = modT[:, i, mA * 8 + kc, 1:17].unsqueeze(1).broadcast_to([128, 4, 16])
                    Sb = modT[:, i, mS * 8 + kc, 1:17].unsqueeze(1).broadcast_to([128, 4, 16])
                    tv = tt[:, 0:NS].rearrange("p (t b) -> p t b", b=16)
                    op('dve', lambda e, tv=tv, Ab=Ab: e.tensor_tensor(out=tv, in0=tv, in1=Ab, op=ALU.mult),
                       reads=[tk, ('mod', i, mA)], writes=[tk])
                    hl = hlast[:, kc, 15:15 + NS].rearrange("p (t b) -> p t b", b=16)
                    op('dve', lambda e, tv=tv, Sb=Sb, hl=hl: e.tensor_tensor(out=hl, in0=tv, in1=Sb, op=ALU.add),
                       reads=[tk, ('mod', i, mS)], writes=['hlast'])
                    op('dve', lambda e, kc=kc, c0=c0: e.tensor_copy(out=hT[:, kc, HOFF + c0:HOFF + c0 + NS], in_=hlast[:, kc, 15:15 + NS]),
                       reads=['hlast'], writes=[('hT', ti)])

    def fm_to_tm(src_fn, src_keys, nch, ncols, store_fn, tag):
        stg = [carve_f32([128, 512]) for _ in range(2)]
        for gi, c0 in enumerate(range(0, nch, 4)):
            n4 = min(4, nch - c0)
            st = stg[gi % 2]
            sk = (tag, gi % 2)
            t_ps, tk = psum()
            for k in range(n4):
                op('pe', lambda e, a=t_ps, k=k, ch=c0 + k: e.transpose(a[0:ncols, k * 128:(k + 1) * 128], src_fn(ch), ident[:, :]),
                   reads=list(src_keys) + ['ident'], writes=[tk])
            op('dve', lambda e, a=t_ps, st=st, n4=n4: e.tensor_copy(out=st[0:ncols, 0:n4 * 128], in_=a[0:ncols, 0:n4 * 128]),
               reads=[tk], writes=[sk])
            store_fn(st, sk, c0, n4)

    def pool_mixer(i, j):
        PW = carve_bf16([128, 16, 256])
        dma('pool', PW[:, j * 8:(j + 1) * 8, :], pool_w[j].rearrange("g (kc p) n -> p (g kc) n", p=128), writes=['PW'])
        dT = [carve_bf16([128, KC, 512]) for _ in range(2)]
        tA = [carve_f32([128, 528]) for _ in range(2)]
        tB = [carve_f32([128, 528]) for _ in range(2)]
        op('dve', lambda e: e.memset(hT[:, :, 0:HOFF], 0.0), writes=[('hTpad',)], arena_use=False)
        def pstore(st, sk, c0, n4):
            dma('act', pool_p[j, :, c0 * 128:(c0 + n4) * 128], st[0:15, 0:n4 * 128], reads=[sk])
            for t in range(4):
                dma('act', pool_s[j, :, 11 + t, c0 * 128:(c0 + n4) * 128], st[15 + 16 * t:15 + 16 * t + 16, 0:n4 * 128], reads=[sk])
        fm_to_tm(lambda ch: hlast[:, ch, :], ['hlast'], KC, 15 + NS, pstore, 'pstg')
        dma('sp', pool_s[j, :, 0:11, :], spool[j, :, 4:15, :], arena_use=False)
        uS = carve_f32([128, KC, 16, 19])
        pstg = [carve_f32([128, D]) for _ in range(2)]
        sp_flat = spool[j].rearrange("b r d -> (b r) d")
        for half, (r0, nr) in enumerate(((0, 128), (128, 112))):
            dma('act', pstg[half][0:nr, :], sp_flat[r0:r0 + nr, :], writes=[('pstg2', half)])
        for kc in range(KC):
            u_ps, uk = psum()
            for half, (r0, nr) in enumerate(((0, 128), (128, 112))):
                op('pe', transpose_to(u_ps[:, r0:r0 + nr], pstg[half][0:nr, kc * 128:(kc + 1) * 128], nr),
                   reads=[('pstg2', half), 'ident'], writes=[uk])
            op('act', lambda e, a=u_ps, kc=kc: e.activation(out=uS[:, kc, :, 0:15], in_=a[:, 0:240].rearrange("p (b r) -> p b r", r=15), func=AF.Copy),
               reads=[uk], writes=['uS'])
            op('dve', lambda e, kc=kc: e.tensor_copy(out=uS[:, kc, :, 15:19], in_=hlast[:, kc, 15:15 + NS].rearrange("p (t b) -> p b t", b=16)),
               reads=['hlast'], writes=['uS'])

        def evac_mix(m_ps, mk, ti, c0, n, ch):
            xk = ('XT', ti)
            if ti < 4:
                G = modT[:, i, 2 * 8 + ch, 0:1]
                op('dve', lambda e, a=m_ps, G=G, ch=ch, c0=c0, n=n: e.scalar_tensor_tensor(
                    out=XT[:, ch, c0:c0 + n], in0=a[:, 0:n], scalar=G, in1=XT[:, ch, c0:c0 + n], op0=ALU.mult, op1=ALU.add),
                   reads=[mk, xk, ('mod', i, 2)], writes=[xk])
            else:
                Gb = modT[:, i, 2 * 8 + ch, 1:17].unsqueeze(1).broadcast_to([128, 4, 16])
                tmpk = ('evtmp',)
                op('dve', lambda e, a=m_ps, Gb=Gb: e.tensor_tensor(
                    out=evtmp[:, 0:NS].rearrange("p (t b) -> p t b", b=16), in0=a[:, 0:NS].rearrange("p (t b) -> p t b", b=16), in1=Gb, op=ALU.mult),
                   reads=[mk, ('mod', i, 2)], writes=[tmpk])
                op('dve', lambda e, ch=ch, c0=c0: e.tensor_tensor(out=XT[:, ch, c0:c0 + NS], in0=XT[:, ch, c0:c0 + NS], in1=evtmp[:, 0:NS], op=ALU.add),
                   reads=[tmpk, xk], writes=[xk])

        evtmp = carve_f32([128, 64])
        sA = carve_f32([128, 2, 16, 19])
        sB = carve_f32([128, 2, 16, 19])
        for ti, (c0, n) in enumerate(TILES):
            dt_ = dT[ti % 2]
            dk = ('dT', ti % 2)
            if ti < 4:
                for g, w in enumerate(POOL_W):
                    for c2 in range(2):
                        ch = 2 * g + c2
                        L = n + 15
                        u = hT[:, ch, HOFF + c0 - 15:HOFF + c0 + n]
                        a = tA[c2]
                        b = tB[c2]
                        ak = ('tA', c2)
                        bk = ('tB', c2)
                        hreads = [('hT', ti), ('hTpad',)] + ([('hT', ti - 1)] if ti > 0 else [])
                        eng = 'dve' if c2 == 0 else 'pool'
                        op(eng, lambda e, a=a, u=u, L=L: e.tensor_tensor(out=a[:, 1:L], in0=u[:, 1:L], in1=u[:, 0:L - 1], op=ALU.add),
                           reads=hreads, writes=[ak])
                        cur, curk, oth, othk = a, ak, b, bk
                        sh = 2
                        lo = 1
                        while sh < w:
                            lo2 = lo + sh
                            op(eng, lambda e, cur=cur, oth=oth, lo2=lo2, sh=sh, L=L: e.tensor_tensor(
                                out=oth[:, lo2:L], in0=cur[:, lo2:L], in1=cur[:, lo2 - sh:L - sh], op=ALU.add),
                               reads=[curk], writes=[othk])
                            cur, curk, oth, othk = oth, othk, cur, curk
                            lo = lo2
                            sh *= 2
                        if eng == 'dve':
                            op(eng, lambda e, cur=cur, u=u, L=L, w=w, dt_=dt_, ch=ch, n=n: e.scalar_tensor_tensor(
                                out=dt_[:, ch, 0:n], in0=cur[:, 15:L], scalar=1.0 / w, in1=u[:, 15:L], op0=ALU.mult, op1=ALU.subtract),
                               reads=[curk] + hreads, writes=[dk])
                        else:
                            op(eng, lambda e, cur=cur, oth=oth, L=L, w=w: e.tensor_scalar_mul(out=oth[:, 15:L], in0=cur[:, 15:L], scalar1=1.0 / w),
                               reads=[curk], writes=[othk])
                            op(eng, lambda e, oth=oth, u=u, L=L, dt_=dt_, ch=ch, n=n: e.tensor_tensor(
                                out=dt_[:, ch, 0:n], in0=oth[:, 15:L], in1=u[:, 15:L], op=ALU.subtract),
                               reads=[othk] + hreads, writes=[dk])
                        if ti == 0:
                            op(eng, lambda e, cur=cur, g=g, oth=oth: e.tensor_tensor(out=oth[:, 0:16], in0=cur[:, 15:31], in1=invc[:, g, :], op=ALU.mult),
                               reads=[curk, 'invc'], writes=[othk])
                            op(eng, lambda e, oth=oth, u=u, dt_=dt_, ch=ch: e.tensor_tensor(out=dt_[:, ch, 0:16], in0=oth[:, 0:16], in1=u[:, 15:31], op=ALU.subtract),
                               reads=[othk] + hreads, writes=[dk])
            else:
                for g, w in enumerate(POOL_W):
                    ch0 = 2 * g
                    a, b = sA, sB
                    ak, bk = ('sA',), ('sB',)
                    u = uS[:, ch0:ch0 + 2, :, :]
                    op('dve', lambda e, a=a, u=u: e.tensor_tensor(out=a[:, :, :, 1:19], in0=u[:, :, :, 1:19], in1=u[:, :, :, 0:18], op=ALU.add),
                       reads=['uS'], writes=[ak])
                    cur, curk, oth, othk = a, ak, b, bk
                    sh, lo = 2, 1
                    while sh < w:
                        lo2 = lo + sh
                        op('dve', lambda e, cur=cur, oth=oth, lo2=lo2, sh=sh: e.tensor_tensor(
                            out=oth[:, :, :, lo2:19], in0=cur[:, :, :, lo2:19], in1=cur[:, :, :, lo2 - sh:19 - sh], op=ALU.add),
                           reads=[curk], writes=[othk])
                        cur, curk, oth, othk = oth, othk, cur, curk
                        lo = lo2
                        sh *= 2
                    for c2 in range(2):
                        op('dve', lambda e, cur=cur, u=u, w=w, dt_=dt_, ch0=ch0, c2=c2: e.scalar_tensor_tensor(
                            out=dt_[:, ch0 + c2, 0:NS].rearrange("p (t b) -> p b t", b=16), in0=cur[:, c2, :, 15:19], scalar=1.0 / w,
                            in1=u[:, c2, :, 15:19], op0=ALU.mult, op1=ALU.subtract),
                           reads=[curk, 'uS'], writes=[dk])
            for g in range(4):
                for oc in range(2):
                    m_ps, mk = psum()
                    for kc2 in range(2):
                        op('pe', lambda e, a=m_ps, g=g, oc=oc, kc2=kc2, dt_=dt_, n=n: e.matmul(
                            a[:, 0:n], lhsT=PW[:, j * 8 + g * 2 + kc2, oc * 128:(oc + 1) * 128], rhs=dt_[:, 2 * g + kc2, 0:n],
                            start=(kc2 == 0), stop=(kc2 == 1)), reads=['PW', dk], writes=[mk])
                    evac_mix(m_ps, mk, ti, c0, n, 2 * g + oc)


    def attn_sample(i, qTs, kTs, Vs, Wo, evtmp):
        MBs = carve_f32([128, 9, 16])
        MBn = carve_f32([128, 3, 256])
        dma('act', MBs, mbs_d.rearrange("p (a b) -> p a b", b=16), writes=['MBs'])
        dma('act', MBn[0:64], mbn_d.rearrange("p (a b) -> p a b", b=256), writes=['MBn'])
        ctl = [carve_bf16([128, 512]) for _ in range(4)]
        KTc = [carve_bf16([128, 2, 128]) for _ in range(2)]
        stS = [carve_f32([128, 128]) for _ in range(2)]
        PTs = [carve_bf16([128, 128]) for _ in range(2)]
        Os = carve_f32([128, 6, NS])
        Ds = carve_f32([128, 2, NS])
        zTs = carve_bf16([128, 6, NS])
        nct = 0
        nst = 0
        for g, (win, dil) in enumerate(DIL):
            acc_ps, acck = psum_hold()
            op('pe', lambda e, a=acc_ps: e.matmul(a[:, 0:512], lhsT=zeros_bf[:, 0:128], rhs=zeros_bf[:, :], start=True, stop=False),
               reads=['zeros'], writes=[acck], arena_use=False)
            for hp in range(2):
                c6 = g * 2 + hp
                s_ps, sk_ = psum()
                op('pe', lambda e, a=s_ps, c6=c6: e.matmul(
                    a[0:NS, 0:128].rearrange("p (h q) -> p h q", q=NS), lhsT=kTs[:, c6, :], rhs=qTs[:, c6, :, :], start=True, stop=True),
                   reads=['qTs', 'kTs'], writes=[sk_])
                st = stS[nst % 2]
                stk = ('stS', nst % 2)
                pt = PTs[nst % 2]
                ptk = ('PTs', nst % 2)
                nst += 1
                op('dve', lambda e, a=s_ps, st=st, g=g, hp=hp: e.tensor_tensor(out=st[0:NS, 0:128], in0=a[0:NS, 0:128], in1=MBn[0:NS, g, hp * 128:hp * 128 + 128], op=ALU.add),
                   reads=[sk_, 'MBn'], writes=[stk])
                op('act', lambda e, st=st, pt=pt: e.activation(out=pt[0:NS, 0:128], in_=st[0:NS, 0:128], func=AF.Exp), reads=[stk], writes=[ptk])
                op('pe', lambda e, a=acc_ps, pt=pt, c6=c6, hp=hp: e.matmul(
                    a[:, hp * 128:hp * 128 + 128], lhsT=Vs[0:NS, c6, :], rhs=pt[0:NS, 0:128], start=False, stop=False),
                   reads=[ptk, 'Vs'], writes=[acck])
                op('pe', lambda e, a=acc_ps, pt=pt, hp=hp: e.matmul(
                    a[:, 256 + hp * 128:256 + hp * 128 + 128], lhsT=ones_bf[0:NS, :], rhs=pt[0:NS, 0:128], start=False, stop=False),
                   reads=[ptk, 'ones'], writes=[acck])
            for b in range(16):
                if g == 0:
                    srcs = [(cw[0][b], 0)]
                else:
                    srcs = [(cw[g][b].rearrange("(r d) c -> d r c", d=dil)[tp], 1 + (g - 1) * 4 + tp) for tp in range(4)]
                for (src, mi) in srcs:
                    ct = ctl[nct % 4]
                    ctk = ('ctl', nct % 4)
                    ktc = KTc[nct % 2]
                    ktk = ('KTc', nct % 2)
                    nct += 1
                    dma('pool', ct, src, writes=[ctk])
                    t_ps, tk = psum()
                    tb = t_ps[:, 0:128].bitcast(BF16)
                    for hp in range(2):
                        op('pe', lambda e, tb=tb, ct=ct, hp=hp: e.transpose(tb[:, hp * 128:hp * 128 + 128], ct[:, hp * 128:hp * 128 + 128], ident_bf[:, :]),
                           reads=[ctk, 'identbf'], writes=[tk])
                    op('dve', lambda e, tb=tb, ktc=ktc: e.tensor_copy(out=ktc, in_=tb.rearrange("p (a b) -> p a b", b=128)), reads=[tk], writes=[ktk])
                    s_ps, sk_ = psum()
                    for hp in range(2):
                        qv = qTs[:, g * 2 + hp, :, :].rearrange("p h (t b) -> p b h t", b=16)[:, b]
                        op('pe', lambda e, a=s_ps, ktc=ktc, hp=hp, qv=qv: e.matmul(
                            a[:, hp * 8:hp * 8 + 8].rearrange("p (h t) -> p h t", t=4), lhsT=ktc[:, hp, :], rhs=qv, start=True, stop=True),
                           reads=[ktk, 'qTs'], writes=[sk_])
                    st = stS[nst % 2]
                    stk = ('stS', nst % 2)
                    pt = PTs[nst % 2]
                    ptk = ('PTs', nst % 2)
                    nst += 1
                    op('dve', lambda e, a=s_ps, st=st, mi=mi: e.tensor_tensor(out=st[:, 0:16], in0=a[:, 0:16], in1=MBs[:, mi, :], op=ALU.add),
                       reads=[sk_, 'MBs'], writes=[stk])
                    op('act', lambda e, st=st, pt=pt: e.activation(out=pt[:, 0:16], in_=st[:, 0:16], func=AF.Exp), reads=[stk], writes=[ptk])
                    for hp in range(2):
                        ov = acc_ps[:, hp * 128:hp * 128 + 128].rearrange("p (h t b) -> p b h t", h=2, t=4, b=16)[:, b]
                        op('pe', lambda e, ov=ov, ct=ct, pt=pt, hp=hp: e.matmul(
                            ov, lhsT=ct[:, 256 + hp * 128:256 + hp * 128 + 128], rhs=pt[:, hp * 8:hp * 8 + 8].rearrange("p (h t) -> p h t", t=4), start=False, stop=False),
                           reads=[ptk, ctk], writes=[acck])
                    dvv = acc_ps[:, 256:512].rearrange("p (h t b) -> p b h t", h=4, t=4, b=16)[:, b]
                    last_ = (b == 15 and mi == srcs[-1][1])
                    op('pe', lambda e, dvv=dvv, pt=pt, last_=last_: e.matmul(dvv, lhsT=ones_bf[:, :], rhs=pt[:, 0:16].rearrange("p (h t) -> p h t", t=4), start=False, stop=last_),
                       reads=[ptk, 'ones'], writes=[acck])
            for hp in range(2):
                for hh in range(2):
                    r0, r1 = hh * 64, hh * 64 + 64
                    op('act', lambda e, a=acc_ps, g=g, hp=hp, hh=hh, r0=r0, r1=r1: e.activation(
                        out=Os[r0:r1, g * 2 + hp, :], in_=a[r0:r1, hp * 128 + hh * 64:hp * 128 + hh * 64 + 64], func=AF.Copy), reads=[acck], writes=['Os'])
                    if g == 0:
                        op('dve', lambda e, a=acc_ps, hp=hp, hh=hh, r0=r0, r1=r1: e.tensor_copy(
                            out=Ds[r0:r1, hp, :], in_=a[r0:r1, 256 + hp * 128 + hh * 64:256 + hp * 128 + hh * 64 + 64]), reads=[acck], writes=['Ds'])
                    else:
                        op('dve', lambda e, a=acc_ps, hp=hp, hh=hh, r0=r0, r1=r1: e.tensor_tensor(
                            out=Ds[r0:r1, hp, :], in0=a[r0:r1, 256 + hp * 128 + hh * 64:256 + hp * 128 + hh * 64 + 64], in1=Ds[r0:r1, hp, :], op=ALU.add),
                           reads=[acck, 'Ds'], writes=['Ds'])
            psum_release()
        op('dve', lambda e: e.reciprocal(out=Ds[:, :, :], in_=Ds[:, :, :]), reads=['Ds'], writes=['Ds'])
        for c6 in range(6):
            op('dve', lambda e, c6=c6: e.tensor_tensor(out=zTs[:, c6, :], in0=Os[:, c6, :], in1=Ds[:, c6 % 2, :], op=ALU.mult), reads=['Os', 'Ds'], writes=['zTs'])
        xk = ('XT', 4)
        for oc in range(KC):
            w_ps, wpk = psum()
            for c6 in range(6):
                op('pe', lambda e, a=w_ps, c6=c6, oc=oc: e.matmul(a[:, 0:NS], lhsT=Wo[:, c6, oc * 128:(oc + 1) * 128], rhs=zTs[:, c6, :], start=(c6 == 0), stop=(c6 == 5)),
                   reads=['Wo', 'zTs'], writes=[wpk])
            Gb = modT[:, i, 2 * 8 + oc, 1:17].unsqueeze(1).broadcast_to([128, 4, 16])
            op('dve', lambda e, a=w_ps, Gb=Gb: e.tensor_tensor(
                out=evtmp[:, 0:NS].rearrange("p (t b) -> p t b", b=16), in0=a[:, 0:NS].rearrange("p (t b) -> p t b", b=16), in1=Gb, op=ALU.mult),
               reads=[wpk, ('mod', i, 2)], writes=['evtmp'])
            op('dve', lambda e, oc=oc: e.tensor_tensor(out=XT[:, oc, NP:NP + NS], in0=XT[:, oc, NP:NP + NS], in1=evtmp[:, 0:NS], op=ALU.add),
               reads=['evtmp', xk], writes=[xk])

    def attn_mixer(i):
        qTs = carve_bf16([128, 6, 2, NS])
        kTs = carve_bf16([128, 6, NS])
        Vs = carve_bf16([128, 6, 128])
        Wo = carve_bf16([128, 6, D])
        evtmp = carve_f32([128, 64])
        mark = arena_state['off']
        Wq = [carve_bf16([128, KC, 128]) for _ in range(2)]
        Wkv = [carve_bf16([128, KC, 256]) for _ in range(2)]
        qT = carve_bf16([128, 2, NP])
        kT = carve_bf16([128, NP])
        Vb = carve_bf16([128, 16, 128])
        kvst = [carve_f32([128, 256]) for _ in range(2)]
        OT = carve_bf16([128, 3, NP])
        DT = carve_f32([128, NP])
        MB = carve_f32([128, 3, 512])
        PT = [carve_bf16([128, 512]) for _ in range(2)]
        stmp = [carve_f32([128, 512]) for _ in range(2)]
        dma('pool', Wo, attn_w_out.rearrange("(c p) n -> p c n", p=128), writes=['Wo'])
        op('pool', lambda e: e.memset(qT[:, :, :], 0.0), writes=[('qT', t_) for t_ in range(4)])
        op('pool', lambda e: e.memset(qTs[:, :, :, :], 0.0), writes=['qTs'])

        def tokset(ap2d, p, ib, d):
            if d == 1:
                return ap2d[:, ib * 128:(ib + 1) * 128]
            return ap2d.rearrange("p (i a d) -> p d i a", d=d, a=128)[:, p, ib, :]

        def tiles_of(g, ib):
            if g == 0:
                return [ib // 4]
            if g == 1:
                return [ib]
            return [0, 1, 2, 3]

        npass = 0
        cntk = 0
        cntp = 0
        for hp in range(2):
            dma('act', MB, mbp_d[hp].rearrange("p (g c) -> p g c", c=512), writes=['MB'])
            for g, (win, dil) in enumerate(DIL):
                wq = Wq[npass % 2]
                wkv = Wkv[npass % 2]
                wk_ = ('attw', npass % 2)
                npass += 1
                qc = g * 256 + hp * 128
                dma('pool', wq, attn_w_in[:, qc:qc + 128].rearrange("(k p) n -> p k n", p=128), writes=[wk_])
                dma('pool', wkv[:, :, 0:128], attn_w_in[:, 768 + qc:768 + qc + 128].rearrange("(k p) n -> p k n", p=128), writes=[wk_])
                dma('pool', wkv[:, :, 128:256], attn_w_in[:, 1536 + qc:1536 + qc + 128].rearrange("(k p) n -> p k n", p=128), writes=[wk_])
                c6 = g * 2 + hp
                for ti, (c0, n) in (enumerate(TILES) if 'noqk' not in DBG else []):
                    q_ps, qk = psum()
                    for kc in range(KC):
                        op('pe', lambda e, a=q_ps, wq=wq, kc=kc, c0=c0, n=n: e.matmul(
                            a[:, 0:n], lhsT=wq[:, kc, :], rhs=hT[:, kc, HOFF + c0:HOFF + c0 + n], start=(kc == 0), stop=(kc == KC - 1)),
                           reads=[wk_, ('hT', ti)], writes=[qk])
                    k_ps, kk = psum()
                    for kc in range(KC):
                        op('pe', lambda e, a=k_ps, wkv=wkv, kc=kc, c0=c0, n=n: e.matmul(
                            a[:, 0:n], lhsT=wkv[:, kc, 0:128], rhs=hT[:, kc, HOFF + c0:HOFF + c0 + n], start=(kc == 0), stop=(kc == KC - 1)),
                           reads=[wk_, ('hT', ti)], writes=[kk])
                    if ti < 4:
                        for hh in range(2):
                            op('act', lambda e, a=q_ps, c0=c0, n=n, hh=hh: e.activation(out=qT[hh * 64:hh * 64 + 64, hh, c0:c0 + n], in_=a[hh * 64:hh * 64 + 64, 0:n], func=AF.Identity, scale=0.125),
                               reads=[qk], writes=[('qT', ti)])
                        op('dve', lambda e, a=k_ps, c0=c0, n=n: e.tensor_copy(out=kT[:, c0:c0 + n], in_=a[:, 0:n]), reads=[kk], writes=[('kT', ti)])
                    else:
                        for hh in range(2):
                            op('act', lambda e, a=q_ps, c6=c6, hh=hh: e.activation(out=qTs[hh * 64:hh * 64 + 64, c6, hh, :], in_=a[hh * 64:hh * 64 + 64, 0:NS], func=AF.Identity, scale=0.125),
                               reads=[qk], writes=['qTs'])
                        op('dve', lambda e, a=k_ps, c6=c6: e.tensor_copy(out=kTs[:, c6, :], in_=a[:, 0:NS]), reads=[kk], writes=['kTs'])
                nph = dil
                nib = 16 // dil
                blocks = [(p, ib) for p in range(nph) for ib in range(nib)]
                for item in ((blocks if 'nokvp' not in DBG else []) + (['sample'] if 'nokvs' not in DBG else [])):
                    kv_ps, kvk = psum()
                    if item == 'sample':
                        m = NS
                        hreads = [('hT', 4)]
                        lfn = lambda kc: hT[:, kc, HOFF + NP:HOFF + NP + NS]
                    else:
                        p, ib = item
                        m = 128
                        hreads = [('hT', t_) for t_ in tiles_of(g, ib)]
                        lfn = lambda kc, p=p, ib=ib, dil=dil: tokset(hT[:, kc, HOFF:HOFF + NP], p, ib, dil)
                    for kc in range(KC):
                        op('pe', lambda e, a=kv_ps, lfn=lfn, wkv=wkv, kc=kc, m=m: e.matmul(
                            a[0:m, 0:256], lhsT=lfn(kc), rhs=wkv[:, kc, :], start=(kc == 0), stop=(kc == KC - 1)),
                           reads=[wk_] + hreads, writes=[kvk])
                    if item == 'sample':
                        if 'nokvs_act' not in DBG:
                            op('act', lambda e, a=kv_ps, c6=c6: e.activation(out=Vs[0:NS, c6, :], in_=a[0:NS, 128:256], func=AF.Copy), reads=[kvk], writes=['Vs'])
                        st = kvst[cntk % 2]
                        sk = ('kvst', cntk % 2)
                        cntk += 1
                        if 'nokvs_dve' not in DBG:
                            op('dve', lambda e, a=kv_ps, st=st: e.tensor_copy(out=st[0:NS, :], in_=a[0:NS, 0:256]), reads=[kvk], writes=[sk])
                        for t in (range(4) if 'nokvstore' not in DBG else []):
                            dma('act', win_s[g][:, win - 4 + t, hp * 128:hp * 128 + 128], st[16 * t:16 * t + 16, 0:128], reads=[sk])
                            dma('act', win_s[g][:, win - 4 + t, 256 + hp * 128:256 + hp * 128 + 128], st[16 * t:16 * t + 16, 128:256], reads=[sk])
                    else:
                        bidx = p * nib + ib
                        op('act', lambda e, a=kv_ps, bidx=bidx: e.activation(out=Vb[:, bidx, :], in_=a[:, 128:256], func=AF.Copy), reads=[kvk], writes=[('Vb', bidx)])
                        if ib == nib - 1 and 'nokvstore' not in DBG:
                            st = kvst[cntk % 2]
                            sk = ('kvst', cntk % 2)
                            cntk += 1
                            op('dve', lambda e, a=kv_ps, st=st: e.tensor_copy(out=st[:, :], in_=a[:, 0:256]), reads=[kvk], writes=[sk])
                            wv = win_p[g] if dil == 1 else win_p[g].rearrange("(a d) c -> d a c", d=dil)[p]
                            dma('act', wv[:, hp * 128:hp * 128 + 128], st[:, 0:128], reads=[sk])
                            dma('act', wv[:, 256 + hp * 128:256 + hp * 128 + 128], st[:, 128:256], reads=[sk])
                for (p, ib) in (blocks if 'noblocks' not in DBG else []):
                    bidx = p * nib + ib
                    has_prev = ib > 0
                    ncol = 512 if has_prev else 256
                    s_ps, sk_ = psum()
                    qv = qT[:, :, ib * 128:(ib + 1) * 128] if dil == 1 else qT.rearrange("p h (i a d) -> p d i h a", d=dil, a=128)[:, p, ib]
                    rk = [('qT', t_) for t_ in tiles_of(g, ib)] + [('kT', t_) for t_ in tiles_of(g, ib)]
                    if has_prev:
                        rk += [('kT', t_) for t_ in tiles_of(g, ib - 1)]
                    for cp in range(2 if has_prev else 1):
                        kv_ = tokset(kT, p, ib - cp, dil)
                        op('pe', lambda e, a=s_ps, kv_=kv_, qv=qv, cp=cp: e.matmul(
                            a[:, cp * 256:cp * 256 + 256].rearrange("p (h a) -> p h a", a=128), lhsT=kv_, rhs=qv,
                            start=True, stop=True), reads=rk, writes=[sk_])
                    stt = stmp[cntp % 2]
                    stk = ('stmp', cntp % 2)
                    pt = PT[cntp % 2]
                    ptk = ('PT', cntp % 2)
                    cntp += 1
                    op('dve', lambda e, a=s_ps, stt=stt, g=g, ncol=ncol: e.tensor_tensor(out=stt[:, 0:ncol], in0=a[:, 0:ncol], in1=MB[:, g, 0:ncol], op=ALU.add),
                       reads=[sk_, 'MB'], writes=[stk])
                    op('act', lambda e, stt=stt, pt=pt, ncol=ncol: e.activation(out=pt[:, 0:ncol], in_=stt[:, 0:ncol], func=AF.Exp), reads=[stk], writes=[ptk])
                    if 'blk_nopv' in DBG:
                        continue
                    o_ps, ok_ = psum()
                    for cp in range(2 if has_prev else 1):
                        op('pe', lambda e, a=o_ps, pt=pt, cp=cp, bb=bidx - cp, hp_=has_prev: e.matmul(
                            a[:, 0:256], lhsT=Vb[:, bb, :], rhs=pt[:, cp * 256:(cp + 1) * 256], start=(cp == 0), stop=(cp == (1 if hp_ else 0))),
                           reads=[ptk, ('Vb', bidx - cp)], writes=[ok_])
                    for cp in range(2 if has_prev else 1):
                        op('pe', lambda e, a=o_ps, pt=pt, cp=cp, hp_=has_prev: e.matmul(
                            a[:, 256:512], lhsT=ones_bf[:, :], rhs=pt[:, cp * 256:(cp + 1) * 256], start=(cp == 0), stop=(cp == (1 if hp_ else 0))),
                           reads=[ptk, 'ones'], writes=[ok_])
                    for hh in (range(2) if 'blk_noevac' not in DBG else []):
                        r0, r1 = hh * 64, hh * 64 + 64
                        ov = tokset(OT[:, g, :], p, ib, dil)
                        dv = tokset(DT, p, ib, dil)
                        op('act', lambda e, a=o_ps, ov=ov, r0=r0, r1=r1, hh=hh: e.activation(out=ov[r0:r1, :], in_=a[r0:r1, hh * 128:hh * 128 + 128], func=AF.Copy),
                           reads=[ok_], writes=[('OT', g)])
                        if g == 0:
                            op('dve', lambda e, a=o_ps, dv=dv, r0=r0, r1=r1, hh=hh: e.tensor_copy(out=dv[r0:r1, :], in_=a[r0:r1, 256 + hh * 128:256 + hh * 128 + 128]),
                               reads=[ok_], writes=['DT'])
                        else:
                            op('dve', lambda e, a=o_ps, dv=dv, r0=r0, r1=r1, hh=hh: e.tensor_tensor(out=dv[r0:r1, :], in0=a[r0:r1, 256 + hh * 128:256 + hh * 128 + 128], in1=dv[r0:r1, :], op=ALU.add),
                               reads=[ok_, 'DT'], writes=['DT'])
            if 'nomerge' in DBG:
                continue
            for c0 in range(0, NP, 512):
                op('dve', lambda e, c0=c0: e.reciprocal(out=DT[:, c0:c0 + 512], in_=DT[:, c0:c0 + 512]), reads=['DT'], writes=['DT'])
                for g in range(3):
                    op('dve' if g < 2 else 'pool', lambda e, c0=c0, g=g: e.tensor_tensor(out=OT[:, g, c0:c0 + 512], in0=OT[:, g, c0:c0 + 512], in1=DT[:, c0:c0 + 512], op=ALU.mult),
                       reads=['DT', ('OT', g)], writes=[('OT', g)])
            for ti, (c0, n) in enumerate(TILES[:4]):
                xk = ('XT', ti)
                for oc in range(KC):
                    w_ps, wpk = psum()
                    for g in range(3):
                        op('pe', lambda e, a=w_ps, g=g, oc=oc, c0=c0, n=n, hp=hp: e.matmul(
                            a[:, 0:n], lhsT=Wo[:, g * 2 + hp, oc * 128:(oc + 1) * 128], rhs=OT[:, g, c0:c0 + n], start=(g == 0), stop=(g == 2)),
                           reads=['Wo', ('OT', g)], writes=[wpk])
                    G = modT[:, i, 2 * 8 + oc, 0:1]
                    op('dve', lambda e, a=w_ps, G=G, oc=oc, c0=c0, n=n: e.scalar_tensor_tensor(
                        out=XT[:, oc, c0:c0 + n], in0=a[:, 0:n], scalar=G, in1=XT[:, oc, c0:c0 + n], op0=ALU.mult, op1=ALU.add),
                       reads=[wpk, xk, ('mod', i, 2)], writes=[xk])
        arena_reset()
        arena_state['off'] = mark
        if 'nosample' not in DBG:
            attn_sample(i, qTs, kTs, Vs, Wo, evtmp)

    def gla_mixer(i):
        Tri = carve_f32([128, 128])
        Msk = carve_f32([128, 128])
        TriS = carve_f32([128, 64])
        MskS = carve_f32([128, 64])
        mkb = carve_f32([128, 16])
        oneT = carve_f32([128, 1])
        dma('act', Tri, gtri_d[:, :], writes=['gconst'])
        dma('act', Msk, gmsk_d[:, :], writes=['gconst'])
        dma('act', TriS[0:NS], gtris_d[:, :], writes=['gconst'])
        dma('act', MskS[0:NS], gmsks_d[:, :], writes=['gconst'])
        dma('act', mkb[0:NS], gmkb_d[:, :], writes=['gconst'])
        op('dve', lambda e: e.memset(oneT[:], 1.0), writes=['oneT'])
        wg = carve_bf16([128, 512])
        dma('pool', wg[0:17, :], gla_wg[:, :], writes=['wg'])
        gdT = carve_bf16([128, 128])
        op('dve', lambda e: e.memset(gdT[0:32, :], 1.0), writes=['gdT'])
        ring = []
        for r_ in range(2):
            ring.append(dict(qk=carve_bf16([128, KC, 256]), v=carve_bf16([128, KC, 256]), r=carve_bf16([128, KC, 256]),
                             gd=carve_bf16([128, KC, 16]), o=carve_bf16([128, 2, D])))
        S = carve_f32([128, 256])
        S_bf = carve_bf16([128, 256])
        sp = carve_f32([128, 128])
        e1 = carve_f32([128, 128])
        E1 = carve_f32([128, 128])
        E2 = carve_f32([128, 128])
        E3 = carve_f32([128, 128])
        bend = carve_f32([128, 16])
        ebend = carve_f32([128, 16])
        qe = carve_bf16([128, 128])
        ke = carve_bf16([128, 128])
        kend = carve_f32([128, 128])
        kendT = carve_bf16([128, 128])
        attm = carve_bf16([128, 128])
        vtok = carve_bf16([128, 256])
        oT = carve_f32([128, 2, 128])
        sq = carve_bf16([128, 2, 128])
        rs = carve_f32([128, 128])
        srAll = carve_bf16([128, 2, NT])
        zT = carve_bf16([128, 2, 128])
        S0f = [carve_f32([128, 256]) for _ in range(2)]
        S0b = [carve_bf16([128, 256]) for _ in range(16)]
        Vm = [carve_bf16([128, 256]) for _ in range(2)]
        evtmp = carve_f32([128, 64])
        for h in range(4):
            W = ring[h % 2]
            wk_ = ('glaw', h % 2)
            dma('pool', W['qk'][:, :, 0:128], gla_w_in[:, h * 128:(h + 1) * 128].rearrange("(k p) n -> p k n", p=128), writes=[wk_])
            dma('pool', W['qk'][:, :, 128:256], gla_w_in[:, 512 + h * 128:512 + (h + 1) * 128].rearrange("(k p) n -> p k n", p=128), writes=[wk_])
            dma('pool', W['v'], gla_w_in[:, 1024 + h * 256:1024 + (h + 1) * 256].rearrange("(k p) n -> p k n", p=128), writes=[wk_])
            dma('pool', W['r'], gla_w_in[:, 2048 + h * 256:2048 + (h + 1) * 256].rearrange("(k p) n -> p k n", p=128), writes=[wk_])
            dma('pool', W['gd'], gla_w_in[:, 3072:3088].rearrange("(k p) n -> p k n", p=128), writes=[wk_])
            dma('pool', W['o'], gla_w_out[h * 256:(h + 1) * 256, :].rearrange("(c p) n -> p c n", p=128), writes=[wk_])
            for ti_, (c0_, n_) in enumerate(TILES):
                for rc in range(2):
                    r_ps, rk = psum()
                    for kc in range(KC):
                        op('pe', lambda e, a=r_ps, rc=rc, kc=kc, n_=n_, c0_=c0_, W=W: e.matmul(
                            a[:, 0:n_], lhsT=W['r'][:, kc, rc * 128:(rc + 1) * 128], rhs=hT[:, kc, HOFF + c0_:HOFF + c0_ + n_], start=(kc == 0), stop=(kc == KC - 1)),
                           reads=[wk_, ('hT', ti_)], writes=[rk])
                    op('act', lambda e, a=r_ps, rc=rc, n_=n_, c0_=c0_: e.activation(out=srAll[:, rc, c0_:c0_ + n_], in_=a[:, 0:n_], func=AF.Silu),
                       reads=[rk], writes=[('sr', ti_)])
            for c in range(17):
                samp = (c == 16)
                n = NS if samp else 128
                c0 = NP if samp else c * 128
                ti = 4 if samp else c // 4
                hk = ('hT', ti)
                xk = ('XT', ti)
                hsl = lambda kc, c0=c0, n=n: hT[:, kc, HOFF + c0:HOFF + c0 + n]
                q_ps, qk = psum()
                k_ps, kk = psum()
                for part, (pp, pk) in enumerate(((q_ps, qk), (k_ps, kk))):
                    for kc in range(KC):
                        op('pe', lambda e, a=pp, part=part, kc=kc, n=n, hsl=hsl, W=W: e.matmul(
                            a[:, 0:n], lhsT=W['qk'][:, kc, part * 128:(part + 1) * 128], rhs=hsl(kc), start=(kc == 0), stop=(kc == KC - 1)),
                           reads=[wk_, hk], writes=[pk])
                g_ps, gk = psum()
                for kc in range(KC):
                    op('pe', lambda e, a=g_ps, kc=kc, n=n, hsl=hsl, W=W: e.matmul(
                        a[0:16, 0:n], lhsT=W['gd'][:, kc, :], rhs=hsl(kc), start=(kc == 0), stop=(kc == KC - 1)), reads=[wk_, hk], writes=[gk])
                op('act', lambda e, a=g_ps, n=n: e.activation(out=gdT[0:16, 0:n], in_=a[0:16, 0:n], func=AF.Copy), reads=[gk], writes=['gdT'])
                v_ps, vk = psum()
                for kc in range(KC):
                    op('pe', lambda e, a=v_ps, kc=kc, n=n, hsl=hsl, W=W: e.matmul(
                        a[0:n, 0:256], lhsT=hsl(kc), rhs=W['v'][:, kc, :], start=(kc == 0), stop=(kc == KC - 1)), reads=[wk_, hk], writes=[vk])
                op('act', lambda e, a=v_ps, n=n: e.activation(out=vtok[0:n, :], in_=a[0:n, 0:256], func=AF.Copy), reads=[vk], writes=['vtok'])
                z_ps, zk = psum()
                op('pe', lambda e, a=z_ps, n=n, h=h: e.matmul(a[0:n, 0:128], lhsT=gdT[0:17, 0:n], rhs=wg[0:17, h * 128:(h + 1) * 128], start=True, stop=True),
                   reads=['gdT', 'wg'], writes=[zk])
                op('act', lambda e, a=z_ps, n=n: e.activation(out=e1[0:n, :], in_=a[0:n, 0:128], func=AF.Exp, scale=-1.0), reads=[zk], writes=['e1'])
                op('act', lambda e, n=n: e.activation(out=sp[0:n, :], in_=e1[0:n, :], func=AF.Ln, bias=oneT[0:n, 0:1], scale=1.0), reads=['e1', 'oneT'], writes=['sp'])
                b_ps, bk = psum()
                tri = TriS if samp else Tri
                op('pe', lambda e, a=b_ps, n=n, tri=tri: e.matmul(a[:, 0:n], lhsT=sp[0:n, :], rhs=tri[0:n, 0:n], start=True, stop=True),
                   reads=['sp', 'gconst'], writes=[bk])
                op('act', lambda e, a=b_ps, n=n: e.activation(out=E1[:, 0:n], in_=a[:, 0:n], func=AF.Exp), reads=[bk], writes=['E1'])
                op('act', lambda e, a=b_ps, n=n: e.activation(out=E2[:, 0:n], in_=a[:, 0:n], func=AF.Exp, scale=-1.0), reads=[bk], writes=['E2'])
                if not samp:
                    op('dve', lambda e, a=b_ps, n=n: e.tensor_copy(out=bend[:, 0:1], in_=a[:, n - 1:n]), reads=[bk], writes=['bend'])
                    op('act', lambda e, a=b_ps, n=n: e.activation(out=E3[:, 0:n], in_=a[:, 0:n], func=AF.Exp, bias=bend[:, 0:1], scale=-1.0), reads=[bk, 'bend'], writes=['E3'])
                    op('act', lambda e: e.activation(out=ebend[:, 0:1], in_=bend[:, 0:1], func=AF.Exp), reads=['bend'], writes=['ebend'])
                else:
                    op('dve', lambda e, a=b_ps: e.tensor_copy(out=bend[:, 0:16], in_=a[:, 48:64]), reads=[bk], writes=['bend'])
                    op('dve', lambda e, a=b_ps: e.tensor_tensor(out=E3[:, 0:NS].rearrange("p (t b) -> p t b", b=16),
                                                                in0=bend[:, 0:16].unsqueeze(1).broadcast_to([128, 4, 16]),
                                                                in1=a[:, 0:NS].rearrange("p (t b) -> p t b", b=16), op=ALU.subtract), reads=[bk, 'bend'], writes=['E3'])
                    op('act', lambda e: e.activation(out=E3[:, 0:NS], in_=E3[:, 0:NS], func=AF.Exp), reads=['E3'], writes=['E3'])
                    op('act', lambda e: e.activation(out=ebend[:, 0:16], in_=bend[:, 0:16], func=AF.Exp), reads=['bend'], writes=['ebend'])
                op('dve', lambda e, a=q_ps, n=n: e.scalar_tensor_tensor(out=qe[:, 0:n], in0=a[:, 0:n], scalar=float(128 ** -0.5), in1=E1[:, 0:n], op0=ALU.mult, op1=ALU.mult),
                   reads=[qk, 'E1'], writes=['qe'])
                op('dve', lambda e, a=k_ps, n=n: e.tensor_tensor(out=ke[:, 0:n], in0=a[:, 0:n], in1=E2[:, 0:n], op=ALU.mult), reads=[kk, 'E2'], writes=['ke'])
                op('dve', lambda e, a=k_ps, n=n: e.tensor_tensor(out=kend[:, 0:n], in0=a[:, 0:n], in1=E3[:, 0:n], op=ALU.mult), reads=[kk, 'E3'], writes=['kend'])
                a_ps, ak = psum()
                op('pe', lambda e, a=a_ps, n=n: e.matmul(a[0:n, 0:n], lhsT=ke[:, 0:n], rhs=qe[:, 0:n], start=True, stop=True), reads=['ke', 'qe'], writes=[ak])
                msk = MskS if samp else Msk
                op('dve', lambda e, a=a_ps, n=n, msk=msk: e.tensor_tensor(out=attm[0:n, 0:n], in0=a[0:n, 0:n], in1=msk[0:n, 0:n], op=ALU.mult),
                   reads=[ak, 'gconst'], writes=['attm'])
                t_ps, tk = psum()
                op('pe', lambda e, a=t_ps, n=n: e.transpose(a[0:n, 0:128], kend[:, 0:n], ident[:, :]), reads=['kend', 'ident'], writes=[tk])
                op('act', lambda e, a=t_ps, n=n: e.activation(out=kendT[0:n, :], in_=a[0:n, 0:128], func=AF.Copy), reads=[tk], writes=['kendT'])
                o_ps, ok_ = psum()
                if samp:
                    sbts = []
                    for b in range(16):
                        sbt = S0b[b]
                        sbk = ('S0b', b)
                        dma('pool', sbt, sgla[b, h], writes=[sbk])
                        sbts.append((sbt, sbk))
                for vc in range(2):
                    only_intra = (c == 0 and not samp)
                    op('pe', lambda e, a=o_ps, vc=vc, n=n, st_=only_intra: e.matmul(a[:, vc * 128:vc * 128 + n], lhsT=vtok[0:n, vc * 128:(vc + 1) * 128], rhs=attm[0:n, 0:n],
                                                                                  start=True, stop=st_), reads=['vtok', 'attm'], writes=[ok_])
                    if samp:
                        for b in range(16):
                            sbt, sbk = sbts[b]
                            ov = o_ps[:, vc * 128:vc * 128 + NS].rearrange("p (t b) -> p b t", b=16)[:, b, :]
                            qv = qe[:, 0:NS].rearrange("p (t b) -> p b t", b=16)[:, b, :]
                            op('pe', lambda e, ov=ov, qv=qv, sbt=sbt, vc=vc, b=b: e.matmul(ov, lhsT=sbt[:, vc * 128:(vc + 1) * 128], rhs=qv, start=False, stop=(b == 15)),
                               reads=[sbk, 'qe'], writes=[ok_])
                    elif c > 0:
                        op('pe', lambda e, a=o_ps, vc=vc, n=n: e.matmul(a[:, vc * 128:vc * 128 + n], lhsT=S_bf[:, vc * 128:(vc + 1) * 128], rhs=qe[:, 0:n],
                                                                       start=False, stop=True), reads=['S_bf', 'qe'], writes=[ok_])
                op('act', lambda e, a=o_ps, n=n: e.activation(out=oT[:, :, 0:n], in_=a[:, 0:256].rearrange("p (a b) -> p a b", b=128)[:, :, 0:n], func=AF.Copy),
                   reads=[ok_], writes=['oT'])
                op('act', lambda e, n=n: e.activation(out=sq[:, :, 0:n], in_=oT[:, :, 0:n], func=AF.Square), reads=['oT'], writes=['sq'])
                n_ps, nk = psum()
                for vc in range(2):
                    op('pe', lambda e, a=n_ps, vc=vc, n=n: e.matmul(a[:, 0:n], lhsT=ones_bf[:, :], rhs=sq[:, vc, 0:n], start=(vc == 0), stop=(vc == 1)),
                       reads=['sq', 'ones'], writes=[nk])
                op('act', lambda e, a=n_ps, n=n: e.activation(out=rs[:, 0:n], in_=a[:, 0:n], func=AF.Ln, bias=epsT[:, 0:1], scale=1.0 / 256), reads=[nk, 'epsT'], writes=['rs'])
                op('act', lambda e, n=n: e.activation(out=rs[:, 0:n], in_=rs[:, 0:n], func=AF.Exp, scale=-0.5), reads=['rs'], writes=['rs'])
                for vc in range(2):
                    gcol = PV_GNG + h * 2 + vc
                    op('dve', lambda e, vc=vc, n=n, gcol=gcol: e.scalar_tensor_tensor(out=oT[:, vc, 0:n], in0=oT[:, vc, 0:n], scalar=PV[:, gcol:gcol + 1], in1=rs[:, 0:n],
                                                                                     op0=ALU.mult, op1=ALU.mult), reads=['oT', 'rs', 'PV'], writes=['oT'])
                    op('dve', lambda e, vc=vc, n=n, c0=c0: e.tensor_tensor(out=zT[:, vc, 0:n], in0=oT[:, vc, 0:n], in1=srAll[:, vc, c0:c0 + n], op=ALU.mult), reads=['oT', ('sr', ti)], writes=['zT'])
                for oc in range(KC):
                    w_ps, wpk = psum()
                    for vc in range(2):
                        op('pe', lambda e, a=w_ps, vc=vc, oc=oc, n=n, W=W: e.matmul(a[:, 0:n], lhsT=W['o'][:, vc, oc * 128:(oc + 1) * 128], rhs=zT[:, vc, 0:n],
                                                                                 start=(vc == 0), stop=(vc == 1)), reads=[wk_, 'zT'], writes=[wpk])
                    if not samp:
                        G = modT[:, i, 2 * 8 + oc, 0:1]
                        op('dve', lambda e, a=w_ps, G=G, oc=oc, c0=c0, n=n: e.scalar_tensor_tensor(
                            out=XT[:, oc, c0:c0 + n], in0=a[:, 0:n], scalar=G, in1=XT[:, oc, c0:c0 + n], op0=ALU.mult, op1=ALU.add),
                           reads=[wpk, xk, ('mod', i, 2)], writes=[xk])
                    else:
                        Gb = modT[:, i, 2 * 8 + oc, 1:17].unsqueeze(1).broadcast_to([128, 4, 16])
                        op('dve', lambda e, a=w_ps, Gb=Gb: e.tensor_tensor(
                            out=evtmp[:, 0:NS].rearrange("p (t b) -> p t b", b=16), in0=a[:, 0:NS].rearrange("p (t b) -> p t b", b=16), in1=Gb, op=ALU.mult),
                           reads=[wpk, ('mod', i, 2)], writes=['evtmp'])
                        op('dve', lambda e, oc=oc: e.tensor_tensor(out=XT[:, oc, NP:NP + NS], in0=XT[:, oc, NP:NP + NS], in1=evtmp[:, 0:NS], op=ALU.add),
                           reads=['evtmp', xk], writes=[xk])
                if not samp:
                    s_ps, sk_ = psum()
                    op('pe', lambda e, a=s_ps, n=n: e.matmul(a[:, 0:256], lhsT=kendT[0:n, :], rhs=vtok[0:n, :], start=True, stop=True), reads=['kendT', 'vtok'], writes=[sk_])
                    if c == 0:
                        op('dve', lambda e, a=s_ps: e.tensor_copy(out=S[:, :], in_=a[:, 0:256]), reads=[sk_], writes=['S'])
                    else:
                        op('dve', lambda e, a=s_ps: e.scalar_tensor_tensor(out=S[:, :], in0=S[:, :], scalar=ebend[:, 0:1], in1=a[:, 0:256], op0=ALU.mult, op1=ALU.add),
                           reads=[sk_, 'S', 'ebend'], writes=['S'])
                    if c < 15:
                        op('act', lambda e: e.activation(out=S_bf[:, :], in_=S[:, :], func=AF.Copy), reads=['S'], writes=['S_bf'])
                    else:
                        dma('act', gla_p[h], S[:, :], reads=['S'])
                else:
                    for b in range(16):
                        sf = S0f[b % 2]
                        sfk = ('S0f', b % 2)
                        vm = Vm[b % 2]
                        vmk = ('Vm', b % 2)
                        dma('act', sf, sgla[b, h], writes=[sfk])
                        op('pool', lambda e, vm=vm, b=b: e.tensor_scalar_mul(out=vm[0:NS, :], in0=vtok[0:NS, :], scalar1=mkb[0:NS, b:b + 1]), reads=['vtok', 'gconst'], writes=[vmk])
                        s_ps, sk_ = psum()
                        op('pe', lambda e, a=s_ps, vm=vm: e.matmul(a[:, 0:256], lhsT=kendT[0:NS, :], rhs=vm[0:NS, :], start=True, stop=True), reads=['kendT', vmk], writes=[sk_])
                        op('dve', lambda e, a=s_ps, sf=sf, b=b: e.scalar_tensor_tensor(out=sf[:, :], in0=sf[:, :], scalar=ebend[:, b:b + 1], in1=a[:, 0:256], op0=ALU.mult, op1=ALU.add),
                           reads=[sk_, sfk, 'ebend'], writes=[sfk])
                        dma('act', gla_s[b, h], sf[:, :], reads=[sfk])

    def conv_ffn(i):
        ring = []
        for s in range(2):
            WG = carve_bf16([128, KC, 512])
            WU = carve_bf16([128, KC, 512])
            WD = carve_bf16([128, 4, D])
            ring.append((WG, WU, WD))
        aT = [carve_bf16([128, 4, 512]) for _ in range(2)]
        gs = [carve_f32([128, 514]) for _ in range(2)]
        acc = [carve_f32([128, 512]) for _ in range(2)]
        gsS = carve_f32([128, NFC, 6 * 16])
        glast = carve_f32([128, NFC, 2 + 32])
        halo = carve_f32([128, NFC, 2])
        evtmp = carve_f32([128, 64])
        cst = [carve_f32([128, 512]) for _ in range(2)]
        for gi, c0_ in enumerate(range(0, NFC, 4)):
            n4 = min(4, NFC - c0_)
            ct = cst[gi % 2]
            ck_ = ('cst', gi % 2)
            for r in range(2):
                dma('act', ct[16 * r:16 * r + 16, 0:n4 * 128], sconv[i, :, r, c0_ * 128:(c0_ + n4) * 128], writes=[ck_])
            t_ps, tk = psum()
            for k in range(n4):
                op('pe', transpose_to(t_ps[:, k * 32:(k + 1) * 32], ct[0:32, k * 128:(k + 1) * 128], 32),
                   reads=[ck_, 'ident'], writes=[tk])
            op('dve', lambda e, a=t_ps, c0_=c0_, n4=n4: e.tensor_copy(
                out=gsS[:, c0_:c0_ + n4, 0:32], in_=a[:, 0:n4 * 32].rearrange("p (a b) -> p a b", b=32)),
               reads=[tk], writes=[('gsS', c0_ + k) for k in range(n4)])
        op('dve', lambda e: e.memset(halo[:], 0.0), writes=[('halo', j_) for j_ in range(NFC)])

        cnt = 0
        for pi_, (j0, nj) in enumerate(FFN_PIECES):
            WG, WU, WD = ring[pi_ % 2]
            wk = ('ffw', pi_ % 2)
            dma('pool', WG[:, :, 0:nj * 128], ffn_w_in[i, :, j0 * 128:(j0 + nj) * 128].rearrange("(k p) n -> p k n", p=128), writes=[wk])
            dma('pool', WU[:, :, 0:nj * 128], ffn_w_in[i, :, DFF + j0 * 128:DFF + (j0 + nj) * 128].rearrange("(k p) n -> p k n", p=128), writes=[wk])
            dma('pool', WD[:, 0:nj, :], ffn_w_down[i, j0 * 128:(j0 + nj) * 128, :].rearrange("(j p) n -> p j n", p=128), writes=[wk])
            for ti, (c0, n) in enumerate(TILES):
                at = aT[cnt % 2]
                atk = ('aT', cnt % 2)
                cnt += 1
                hk = ('hT', ti)
                for jj in range(nj):
                    j = j0 + jj
                    g_ps, gk = psum()
                    u_ps, uk = psum()
                    for kc in range(KC):
                        op('pe', lambda e, a=g_ps, WG=WG, jj=jj, kc=kc, c0=c0, n=n: e.matmul(
                            a[:, 0:n], lhsT=WG[:, kc, jj * 128:(jj + 1) * 128], rhs=hT[:, kc, HOFF + c0:HOFF + c0 + n],
                            start=(kc == 0), stop=(kc == KC - 1)), reads=[wk, hk], writes=[gk])
                    for kc in range(KC):
                        op('pe', lambda e, a=u_ps, WU=WU, jj=jj, kc=kc, c0=c0, n=n: e.matmul(
                            a[:, 0:n], lhsT=WU[:, kc, jj * 128:(jj + 1) * 128], rhs=hT[:, kc, HOFF + c0:HOFF + c0 + n],
                            start=(kc == 0), stop=(kc == KC - 1)), reads=[wk, hk], writes=[uk])
                    w0 = pv(PV_CW + i * 66 + 0 * NFC + j)
                    w1 = pv(PV_CW + i * 66 + 1 * NFC + j)
                    w2 = pv(PV_CW + i * 66 + 2 * NFC + j)
                    cb = pv(PV_CB + i * NFC + j)
                    ac = acc[j % 2]
                    ack = ('acc', j % 2)
                    if ti < 4:
                        gt = gs[j % 2]
                        gtk = ('gs', j % 2)
                        op('act', lambda e, gt=gt, a=g_ps, n=n: e.activation(out=gt[:, 2:2 + n], in_=a[:, 0:n], func=AF.Copy),
                           reads=[gk], writes=[gtk])
                        op('dve', lambda e, gt=gt, j=j: e.tensor_copy(out=gt[:, 0:2], in_=halo[:, j, :]), reads=[('halo', j)], writes=[gtk])
                        op('dve', lambda e, gt=gt, j=j, n=n: e.tensor_copy(out=halo[:, j, :], in_=gt[:, n:n + 2]), reads=[gtk], writes=[('halo', j)])
                        if ti == 3:
                            op('dve', lambda e, gt=gt, j=j, n=n: e.tensor_copy(out=glast[:, j, 0:2], in_=gt[:, n:n + 2]), reads=[gtk], writes=[('glast', j)])
                        op('dve', lambda e, ac=ac, gt=gt, w0=w0, cb=cb, n=n: e.tensor_scalar(
                            out=ac[:, 0:n], in0=gt[:, 0:n], scalar1=w0, scalar2=cb, op0=ALU.mult, op1=ALU.add),
                           reads=[gtk, 'PV'], writes=[ack])
                        op('dve', lambda e, ac=ac, gt=gt, w1=w1, n=n: e.scalar_tensor_tensor(
                            out=ac[:, 0:n], in0=gt[:, 1:1 + n], scalar=w1, in1=ac[:, 0:n], op0=ALU.mult, op1=ALU.add),
                           reads=[gtk, ack, 'PV'], writes=[ack])
                        op('dve', lambda e, ac=ac, a=g_ps, w2=w2, n=n: e.scalar_tensor_tensor(
                            out=ac[:, 0:n], in0=a[:, 0:n], scalar=w2, in1=ac[:, 0:n], op0=ALU.mult, op1=ALU.add),
                           reads=[gk, ack, 'PV'], writes=[ack])
                    else:
                        op('act', lambda e, a=g_ps, j=j: e.activation(out=gsS[:, j, 32:96], in_=a[:, 0:NS], func=AF.Copy),
                           reads=[gk], writes=[('gsS', j)])
                        op('dve', lambda e, j=j: e.tensor_copy(out=glast[:, j, 2:34], in_=gsS[:, j, 64:96]), reads=[('gsS', j)], writes=[('glast', j)])
                        op('dve', lambda e, ac=ac, j=j, w0=w0, cb=cb: e.tensor_scalar(
                            out=ac[:, 0:NS], in0=gsS[:, j, 0:64], scalar1=w0, scalar2=cb, op0=ALU.mult, op1=ALU.add),
                           reads=[('gsS', j), 'PV'], writes=[ack])
                        op('dve', lambda e, ac=ac, j=j, w1=w1: e.scalar_tensor_tensor(
                            out=ac[:, 0:NS], in0=gsS[:, j, 16:80], scalar=w1, in1=ac[:, 0:NS], op0=ALU.mult, op1=ALU.add),
                           reads=[('gsS', j), ack, 'PV'], writes=[ack])
                        op('dve', lambda e, ac=ac, j=j, w2=w2: e.scalar_tensor_tensor(
                            out=ac[:, 0:NS], in0=gsS[:, j, 32:96], scalar=w2, in1=ac[:, 0:NS], op0=ALU.mult, op1=ALU.add),
                           reads=[('gsS', j), ack, 'PV'], writes=[ack])
                    op('act', lambda e, ac=ac, n=n: e.activation(out=ac[:, 0:n], in_=ac[:, 0:n], func=AF.Gelu), reads=[ack], writes=[ack])
                    op('dve', lambda e, ac=ac, a=u_ps, at=at, jj=jj, n=n: e.tensor_tensor(out=at[:, jj, 0:n], in0=ac[:, 0:n], in1=a[:, 0:n], op=ALU.mult),
                       reads=[ack, uk], writes=[atk])
                xk = ('XT', ti)
                for oc in range(KC):
                    d_ps, dk = psum()
                    for jj in range(nj):
                        op('pe', lambda e, a=d_ps, WD=WD, jj=jj, oc=oc, at=at, n=n, nj=nj: e.matmul(
                            a[:, 0:n], lhsT=WD[:, jj, oc * 128:(oc + 1) * 128], rhs=at[:, jj, 0:n],
                            start=(jj == 0), stop=(jj == nj - 1)), reads=[wk, atk], writes=[dk])
                    if ti < 4:
                        G = modT[:, i, 5 * 8 + oc, 0:1]
                        op('dve', lambda e, a=d_ps, G=G, oc=oc, c0=c0, n=n: e.scalar_tensor_tensor(
                            out=XT[:, oc, c0:c0 + n], in0=a[:, 0:n], scalar=G, in1=XT[:, oc, c0:c0 + n], op0=ALU.mult, op1=ALU.add),
                           reads=[dk, xk, ('mod', i, 5)], writes=[xk])
                    else:
                        Gb = modT[:, i, 5 * 8 + oc, 1:17].unsqueeze(1).broadcast_to([128, 4, 16])
                        op('dve', lambda e, a=d_ps, Gb=Gb: e.tensor_tensor(
                            out=evtmp[:, 0:NS].rearrange("p (t b) -> p t b", b=16), in0=a[:, 0:NS].rearrange("p (t b) -> p t b", b=16), in1=Gb, op=ALU.mult),
                           reads=[dk, ('mod', i, 5)], writes=['evtmp'])
                        op('dve', lambda e, oc=oc, c0=c0: e.tensor_tensor(out=XT[:, oc, c0:c0 + NS], in0=XT[:, oc, c0:c0 + NS], in1=evtmp[:, 0:NS], op=ALU.add),
                           reads=['evtmp', xk], writes=[xk])
        def cstore(st, sk, c0, n4):
            dma('act', conv_p[i, :, c0 * 128:(c0 + n4) * 128], st[0:2, 0:n4 * 128], reads=[sk])
            for r in range(2):
                dma('act', conv_s[i, :, r, c0 * 128:(c0 + n4) * 128], st[2 + 16 * r:2 + 16 * r + 16, 0:n4 * 128], reads=[sk])
        fm_to_tm(lambda ch: glast[:, ch, :], [('glast', j_) for j_ in range(NFC)], NFC, 34, cstore, 'cstg')

    def final_out():
        norm_mod(0, 0, final=True)
        ystg = [carve_f32([128, D]) for _ in range(3)]
        for blk in range(17):
            st = ystg[blk % 3]
            sk = ('ystg', blk % 3)
            rows = 128 if blk < 16 else NS
            for q in range(2):
                y_ps, yk = psum()
                for k4 in range(4):
                    kc = q * 4 + k4
                    op('pe', lambda e, a=y_ps, k4=k4, kc=kc, blk=blk, rows=rows: e.transpose(
                        a[0:rows, k4 * 128:(k4 + 1) * 128], XT[:, kc, blk * 128:blk * 128 + rows], ident[:, :]),
                       reads=[('XT', min(blk // 4, 4)), 'ident'], writes=[yk])
                if q == 0:
                    op('dve', lambda e, a=y_ps, st=st, rows=rows: e.tensor_copy(out=st[0:rows, 0:512], in_=a[0:rows, :]), reads=[yk], writes=[sk])
                else:
                    op('act', lambda e, a=y_ps, st=st, rows=rows: e.activation(out=st[0:rows, 512:1024], in_=a[0:rows, :], func=AF.Copy), reads=[yk], writes=[sk])
            if blk < 16:
                dma('sp', yp[blk * 128:(blk + 1) * 128, :], st[:, :], reads=[sk])
            else:
                for t in range(4):
                    dma('sp', ys[:, t, :], st[16 * t:16 * t + 16, :], reads=[sk])

    for i in range(DEPTH):
        kind, j = i % 3, i // 3
        arena_reset()
        if kind == 0:
            norm_mod(i, 0, want_last=True)
            pool_mixer(i, j)
        elif kind == 1:
            norm_mod(i, 0)
            arena_reset()
            if 'noattn' not in DBG:
                attn_mixer(i)
        else:
            norm_mod(i, 0)
            arena_reset()
            gla_mixer(i)
        arena_reset()
        norm_mod(i, 1)
        arena_reset()
        conv_ffn(i)
        if stage <= i:
            break
    arena_reset()
    final_out()
    dma('sp', win_s[2][:, 0:2044, :], cw[2][:, 4:2048, :], arena_use=False)

    P.emit(nc, es)
    es.close()
    return nc


def _t5_bucket(dist):
    nb, md = 32, 2048
    me = nb // 2
    d_f = np.maximum(dist, 1).astype(np.float32)
    large = me + (np.log(d_f / me) / math.log(md / me) * (nb - me)).astype(np.int32)
    large = np.minimum(large, nb - 1)
    return np.where(dist < me, dist, large)


def _attn_consts(rel_bias):
    rb = np.asarray(rel_bias, np.float32)
    vec = np.zeros((3, 4, 129), np.float32)
    for g, (win, dil) in enumerate(DIL):
        bk = _t5_bucket(np.arange(129) * dil)
        vec[g] = rb[bk][:, g * 4:(g + 1) * 4].T
    c = np.arange(128)[:, None]
    a = np.arange(128)[None, :]
    mbp = np.full((2, 128, 3, 2, 2, 128), NEG, np.float32)
    for g in range(3):
        for hp in range(2):
            for hh in range(2):
                h = hp * 2 + hh
                mbp[hp, :, g, 0, hh, :] = np.where(a >= c, vec[g, h][np.clip(a - c, 0, 128)], NEG)
                mbp[hp, :, g, 1, hh, :] = np.where(c >= a, vec[g, h][np.clip(128 + a - c, 0, 128)], NEG)
    r = np.arange(128)
    mbs = np.full((128, 9, 4, 4), NEG, np.float32)
    for h in range(4):
        for t in range(4):
            mbs[:, 0, h, t] = np.where(r >= t, vec[0, h][np.clip(128 + t - r, 0, 128)], NEG)
            for g in (1, 2):
                mbs[:, 1 + (g - 1) * 4 + t, h, t] = vec[g, h][128 - r]
    mbn = np.full((4, 16, 3, 2, 2, 4, 16), NEG, np.float32)
    for g in range(3):
        for hp in range(2):
            for hh in range(2):
                h = hp * 2 + hh
                for b in range(16):
                    for t in range(4):
                        for tp in range(4):
                            if g == 0 and tp <= t:
                                mbn[tp, b, g, hp, hh, t, b] = vec[0, h][t - tp]
                            elif g > 0 and tp == t:
                                mbn[tp, b, g, hp, hh, t, b] = vec[g, h][0]
    return dict(mbp=mbp.reshape(2, 128, 1536), mbs=mbs.reshape(128, 144), mbn=mbn.reshape(64, 768))


def _host_consts():
    ident = np.eye(128, dtype=np.float32)
    invc = np.zeros((128, 4, 16), np.float32)
    for g, w in enumerate(POOL_W):
        for t in range(16):
            invc[:, g, t] = 1.0 / min(t + 1, w)
    sidx = np.arange(128)
    gtri = np.where(sidx[:, None] <= sidx[None, :], -1.0 / 16.0, 0.0).astype(np.float32)
    gmsk = (sidx[:, None] <= sidx[None, :]).astype(np.float32)
    tt = np.arange(64) // 16
    bb = np.arange(64) % 16
    same = bb[:, None] == bb[None, :]
    caus = tt[:, None] <= tt[None, :]
    gtris = np.where(same & caus, -1.0 / 16.0, 0.0).astype(np.float32)
    gmsks = (same & caus).astype(np.float32)
    gmkb = (bb[:, None] == np.arange(16)[None, :]).astype(np.float32)
    return dict(ident_in=ident, invc_in=invc, gtri=gtri, gmsk=gmsk, gtris=gtris, gmsks=gmsks, gmkb=gmkb)


def kernel(x_prompt, x_sample, state_pool, cache_win_g1, cache_win_g2, cache_win_g3, state_gla, state_ffn_conv,
           c_prompt, c_sample, w_ada, b_ada, norm_gain, final_gain, rel_bias, pool_w, pool_scale,
           attn_w_in, attn_w_out, gla_w_in, gla_w_gate_up, gla_b_gate, gla_norm_gain, gla_w_out,
           ffn_w_in, ffn_conv_w, ffn_conv_b, ffn_w_down, _stage=99):
    f = lambda a: np.ascontiguousarray(np.asarray(a, dtype=np.float32))
    nc = build_program(_stage)
    consts = _host_consts()
    consts.update(_attn_consts(rel_bias))
    pvec = np.concatenate([
        f(norm_gain).reshape(64, 128), f(final_gain).reshape(8, 128), f(pool_scale).reshape(16, 128),
        f(gla_norm_gain).reshape(8, 128), f(b_ada).reshape(192, 128), f(ffn_conv_w).reshape(264, 128),
        f(ffn_conv_b).reshape(88, 128)], axis=0)
    gla_wg = np.concatenate([f(gla_w_gate_up)[0], f(gla_b_gate)[0][None, :]], axis=0)
    shared = dict(w_ada=f(w_ada), pvec=f(pvec), pool_w=f(pool_w), attn_w_in=f(attn_w_in)[0], attn_w_out=f(attn_w_out)[0],
                  gla_w_in=f(gla_w_in)[0], gla_wg=f(gla_wg), gla_w_out=f(gla_w_out)[0], ffn_w_in=f(ffn_w_in),
                  ffn_w_down=f(ffn_w_down), **consts)
    in_maps = []
    for c in range(NCORES):
        s = slice(16 * c, 16 * c + 16)
        m = dict(shared)
        m.update(
            xp=f(x_prompt[c]), xs=f(x_sample[s]), spool=f(state_pool[:, s]),
            cw1=f(cache_win_g1[0, s]).reshape(16, 128, 512), cw2=f(cache_win_g2[0, s]).reshape(16, 512, 512),
            cw3=f(cache_win_g3[0, s]).reshape(16, 2048, 512), sgla=f(state_gla[0, s]), sconv=f(state_ffn_conv[:, s]),
            cvec=f(np.concatenate([np.asarray(c_prompt)[c:c + 1], np.asarray(c_sample)[s]], axis=0)))
        in_maps.append(m)
    if 'onecore' in DBG:
        res = run_bass_kernel_spmd(nc, in_maps[:1], core_ids=[0])
        R = list(res.results) * NCORES
    else:
        res = run_bass_kernel_spmd(nc, in_maps, core_ids=list(range(NCORES)))
        R = res.results
    cat = lambda name, axis=0: np.concatenate([np.asarray(r[name]) for r in R], axis=axis)
    stack = lambda name, axis=0: np.stack([np.asarray(r[name]) for r in R], axis=axis)
    y_prompt = stack("yp")
    y_sample = cat("ys")
    pool_p = stack("pool_p", 1)
    pool_s = cat("pool_s", 1)
    outs = [y_prompt, y_sample, pool_p, pool_s]
    for g, (win, dil) in enumerate(DIL):
        outs.append(stack("win%d_p" % (g + 1)).reshape(1, 8, win, 2, 4, 64))
        outs.append(cat("win%d_s" % (g + 1)).reshape(1, 128, win, 2, 4, 64))
    outs.append(stack("gla_p")[None])
    outs.append(cat("gla_s")[None])
    outs.append(stack("conv_p", 1))
    outs.append(cat("conv_s", 1))
    return tuple(np.ascontiguousarray(o, dtype=np.float32) for o in outs)
```
